# Optimizing a Trainium2 kernel written in Bass

```python
import math
import jax, jax.numpy as jnp
from jax import lax
import numpy as np

D_MODEL = 1024
BATCH = 4
SEQ = 8192
DEPTH = 2

BLOCK = 128
N_EVEN = (DEPTH + 1) // 2
N_ODD = DEPTH // 2
DA_HEADS = 4
DA_HEAD_DIM = 64
DA_V_DIM = 2 * DA_HEAD_DIM
DA_WIDTH = DA_HEADS * DA_V_DIM
SB_HEADS = 8
SB_HEAD_DIM = 64
SB_WIDTH = SB_HEADS * SB_HEAD_DIM
MIX_WIDTH = DA_WIDTH + SB_WIDTH
IN_AB = 3 * DA_WIDTH + 3 * SB_WIDTH
SGU_CHUNK = 128
SGU_GROUPS = 8
SGU_WIDTH = D_MODEL
SGU_GROUP_CH = SGU_WIDTH // SGU_GROUPS
D_FF = 2816
EPS = 1e-6

kernel_name = "hybrid_diff_stickbreak_sgu_macaron"


def rms_norm(x, g):
    xf = x.astype(jnp.float32)
    y = xf * lax.rsqrt(jnp.mean(xf * xf, axis=-1, keepdims=True) + EPS)
    return (y * g.astype(jnp.float32)).astype(x.dtype)


def layer_norm(x, g, b):
    xf = x.astype(jnp.float32)
    mu = jnp.mean(xf, axis=-1, keepdims=True)
    xc = xf - mu
    y = xc * lax.rsqrt(jnp.mean(xc * xc, axis=-1, keepdims=True) + EPS)
    return (y * g.astype(jnp.float32) + b.astype(jnp.float32)).astype(x.dtype)


def swiglu(h, w_gate, w_up, w_down):
    return (jax.nn.silu(h @ w_gate) * (h @ w_up)) @ w_down


def alibi_slopes(n):
    return 2.0 ** (-8.0 * jnp.arange(1, n + 1, dtype=jnp.float32) / n)


def diff_stick_mixer(h, w_in, w_out, lam_p, g_subln, lambda_init):
    B, S, _ = h.shape
    nb = S // BLOCK
    proj = h @ w_in
    cuts = [DA_WIDTH, 2 * DA_WIDTH, 3 * DA_WIDTH, 3 * DA_WIDTH + SB_WIDTH, 3 * DA_WIDTH + 2 * SB_WIDTH]
    qa, ka, va, qb, kb, vb = jnp.split(proj, cuts, axis=-1)
    qa = qa.reshape(B, S, DA_HEADS, 2, DA_HEAD_DIM).transpose(0, 2, 3, 1, 4) * (DA_HEAD_DIM ** -0.5)
    ka = ka.reshape(B, S, DA_HEADS, 2, DA_HEAD_DIM).transpose(0, 2, 3, 1, 4)
    va = va.reshape(B, S, DA_HEADS, DA_V_DIM).transpose(0, 2, 1, 3)
    qb = qb.reshape(B, S, SB_HEADS, SB_HEAD_DIM).transpose(0, 2, 1, 3) * (SB_HEAD_DIM ** -0.5)
    kb = kb.reshape(B, S, SB_HEADS, SB_HEAD_DIM).transpose(0, 2, 1, 3)
    vb = vb.reshape(B, S, SB_HEADS, SB_HEAD_DIM).transpose(0, 2, 1, 3)

    lp = lam_p.astype(jnp.float32)
    lam = jnp.exp(jnp.sum(lp[0] * lp[1])) - jnp.exp(jnp.sum(lp[2] * lp[3])) + lambda_init
    slopes = alibi_slopes(DA_HEADS)
    key_pos = jnp.arange(S, dtype=jnp.int32)

    def block(i):
        start = i * BLOCK
        q_pos = start + jnp.arange(BLOCK, dtype=jnp.int32)
        rel = q_pos[:, None] - key_pos[None, :]
        qa_i = lax.dynamic_slice_in_dim(qa, start, BLOCK, axis=3)
        sa = jnp.einsum('bhmqd,bhmkd->bhmqk', qa_i, ka, preferred_element_type=jnp.float32)
        sa = sa - slopes[None, :, None, None, None] * rel.astype(jnp.float32)
        sa = jnp.where(rel >= 0, sa, -jnp.inf)
        p = jax.nn.softmax(sa, axis=-1)
        diff = p[:, :, 0] - lam * p[:, :, 1]
        oa = jnp.einsum('bhqk,bhkd->bhqd', diff.astype(va.dtype), va)
        qb_i = lax.dynamic_slice_in_dim(qb, start, BLOCK, axis=2)
        z = jnp.einsum('bhqd,bhkd->bhqk', qb_i, kb, preferred_element_type=jnp.float32)
        strict = rel > 0
        log_fail = jnp.where(strict, jax.nn.log_sigmoid(-z), 0.0)
        log_suffix = lax.cumsum(log_fail, axis=log_fail.ndim - 1, reverse=True) - log_fail
        wgt = jnp.where(strict, jnp.exp(jax.nn.log_sigmoid(z) + log_suffix), 0.0)
        ob = jnp.einsum('bhqk,bhkd->bhqd', wgt.astype(vb.dtype), vb)
        return oa, ob

    oa, ob = lax.map(block, jnp.arange(nb, dtype=jnp.int32))
    oa = oa.transpose(1, 0, 3, 2, 4).reshape(B, S, DA_HEADS, DA_V_DIM)
    ob = ob.transpose(1, 0, 3, 2, 4).reshape(B, S, SB_HEADS, SB_HEAD_DIM)
    oa = rms_norm(oa, g_subln) * (1.0 - lambda_init)
    o = jnp.concatenate([oa.reshape(B, S, DA_WIDTH), ob.reshape(B, S, SB_WIDTH)], axis=-1)
    return o @ w_out


def sgu_mixer(h, w_uv, b_uv, g_ln, b_ln, w_sp, b_sp, w_out):
    B, S, _ = h.shape
    zz = jax.nn.gelu(h @ w_uv + b_uv)
    u, v = jnp.split(zz, 2, axis=-1)
    v = layer_norm(v, g_ln, b_ln)
    v = v.reshape(B, S // SGU_CHUNK, SGU_CHUNK, SGU_GROUPS, SGU_GROUP_CH)
    causal = jnp.tril(jnp.ones((SGU_CHUNK, SGU_CHUNK), dtype=bool))
    w = jnp.where(causal[None], w_sp, 0.0).astype(v.dtype)
    mixed = jnp.einsum('gts,bnsgc->bntgc', w, v) + b_sp.T[None, None, :, :, None]
    out = u * mixed.reshape(B, S, SGU_WIDTH)
    return out @ w_out


def setup_inputs(seed: int = 0) -> dict:
    key = jax.random.key(seed)
    ks = jax.random.split(key, 16)
    n = jax.random.normal
    f32 = jnp.float32
    return {
        "x": n(ks[0], (BATCH, SEQ, D_MODEL), f32),
        "g_norm": 1.0 + 0.05 * n(ks[1], (DEPTH, 6, D_MODEL), f32),
        "w_ffn_gate": n(ks[2], (DEPTH, 2, D_MODEL, D_FF), f32) * D_MODEL ** -0.5,
        "w_ffn_up": n(ks[3], (DEPTH, 2, D_MODEL, D_FF), f32) * D_MODEL ** -0.5,
        "w_ffn_down": n(ks[4], (DEPTH, 2, D_FF, D_MODEL), f32) * D_FF ** -0.5,
        "w_in_ab": n(ks[5], (N_EVEN, D_MODEL, IN_AB), f32) * D_MODEL ** -0.5,
        "w_out_ab": n(ks[6], (N_EVEN, MIX_WIDTH, D_MODEL), f32) * MIX_WIDTH ** -0.5,
        "lambda_params": 0.1 * n(ks[7], (N_EVEN, 4, DA_HEAD_DIM), f32),
        "g_subln": 1.0 + 0.05 * n(ks[8], (N_EVEN, DA_V_DIM), f32),
        "w_uv": n(ks[9], (N_ODD, D_MODEL, 2 * SGU_WIDTH), f32) * D_MODEL ** -0.5,
        "b_uv": 0.02 * n(ks[10], (N_ODD, 2 * SGU_WIDTH), f32),
        "g_sgu_ln": 1.0 + 0.05 * n(ks[11], (N_ODD, SGU_WIDTH), f32),
        "b_sgu_ln": 0.02 * n(ks[12], (N_ODD, SGU_WIDTH), f32),
        "w_spatial": n(ks[13], (N_ODD, SGU_GROUPS, SGU_CHUNK, SGU_CHUNK), f32) * SGU_CHUNK ** -0.5,
        "b_spatial": 1.0 + 0.05 * n(ks[14], (N_ODD, SGU_GROUPS, SGU_CHUNK), f32),
        "w_out_c": n(ks[15], (N_ODD, SGU_WIDTH, D_MODEL), f32) * SGU_WIDTH ** -0.5,
    }


def reference(x, g_norm, w_ffn_gate, w_ffn_up, w_ffn_down, w_in_ab, w_out_ab,
              lambda_params, g_subln, w_uv, b_uv, g_sgu_ln, b_sgu_ln,
              w_spatial, b_spatial, w_out_c):
    for l in range(DEPTH):
        g = g_norm[l]
        f = swiglu(rms_norm(x, g[0]), w_ffn_gate[l, 0], w_ffn_up[l, 0], w_ffn_down[l, 0])
        x = x + 0.5 * rms_norm(f, g[1])
        h = rms_norm(x, g[2])
        if l % 2 == 0:
            e = l // 2
            lambda_init = 0.8 - 0.6 * math.exp(-0.3 * l)
            m = diff_stick_mixer(h, w_in_ab[e], w_out_ab[e], lambda_params[e], g_subln[e], lambda_init)
        else:
            o = l // 2
            m = sgu_mixer(h, w_uv[o], b_uv[o], g_sgu_ln[o], b_sgu_ln[o], w_spatial[o], b_spatial[o], w_out_c[o])
        x = x + rms_norm(m, g[3])
        f = swiglu(rms_norm(x, g[4]), w_ffn_gate[l, 1], w_ffn_up[l, 1], w_ffn_down[l, 1])
        x = x + 0.5 * rms_norm(f, g[5])
    return x
```

```python
import numpy as np
import concourse.bass as bass
import concourse.mybir as mybir
from concourse.bass_utils import run_bass_kernel_spmd
from contextlib import ExitStack

F32 = mybir.dt.float32
BF16 = mybir.dt.bfloat16
AF = mybir.ActivationFunctionType
ALU = mybir.AluOpType
AX = mybir.AxisListType

D = 1024
KC = 8
FF = 2816
FC = 22
EPS = 1e-6
NEG = -30000.0
SLOPES = [2.0 ** (-8.0 * (i + 1) / 4) for i in range(4)]
LAMBDA_INIT0 = 0.8 - 0.6 * float(np.exp(-0.3 * 0))
ENGS = ("pe", "act", "dve", "pool", "sp")


class Sched:
    def __init__(self, nc, same_engine_sync=("act", "dve", "pool")):
        self.nc = nc
        self.ops = []
        self.last_w = {}
        self.readers = {}
        self.same_engine_sync = set(same_engine_sync)
        self.fence_idx = None

    def op(self, eng, fn, reads=(), writes=(), key=None):
        i = len(self.ops)
        deps = set()
        for r in reads:
            w = self.last_w.get(r)
            if w is not None:
                deps.add(w)
        for w_ in writes:
            w = self.last_w.get(w_)
            if w is not None:
                deps.add(w)
            deps.update(self.readers.get(w_, ()))
        for r in reads:
            self.readers.setdefault(r, []).append(i)
        for w_ in writes:
            self.last_w[w_] = i
            self.readers[w_] = []
        if self.fence_idx is not None:
            deps.add(self.fence_idx)
        deps.discard(i)
        self.ops.append(dict(eng=eng, fn=fn, deps=sorted(deps), key=key))
        return i

    def fence(self):
        allres = set(self.last_w.keys()) | set(self.readers.keys())
        allres.add("__fence__")
        self.fence_idx = self.op("sp", "FENCE", reads=(), writes=sorted(allres, key=str))

    def emit(self):
        nc = self.nc
        ops = self.ops
        need_signal = [False] * len(ops)
        key_cnt = {}
        snaps = []
        for i, o in enumerate(ops):
            snap = {}
            o["deps"] = [d for d in o["deps"] if not (o["key"] is not None and ops[d]["key"] == o["key"])]
            for d in o["deps"]:
                od = ops[d]
                if od["key"] is not None:
                    snap[od["key"]] = key_cnt[od["key"]]
                else:
                    if od["eng"] != o["eng"] or od["eng"] in self.same_engine_sync:
                        need_signal[d] = True
            snaps.append(snap)
            if o["key"] is not None:
                key_cnt[o["key"]] = key_cnt.get(o["key"], 0) + 1
        sig_ord = [0] * len(ops)
        cnt = {e: 0 for e in ENGS}
        for i, o in enumerate(ops):
            if o["key"] is None and need_signal[i]:
                cnt[o["eng"]] += 1
                sig_ord[i] = cnt[o["eng"]]
        keys = sorted({o["key"] for o in ops if o["key"] is not None}, key=str)
        self.n_sems = len(keys) + len(ENGS)
        with ExitStack() as es:
            esem = {e: es.enter_context(nc.semaphore("e_" + e)) for e in ENGS}
            ksem = {k: es.enter_context(nc.semaphore("k_%d" % n)) for n, k in enumerate(keys)}
            block = es.enter_context(nc.Block())

            def make(engname):
                def body(eng):
                    waited = {}
                    for i, o in enumerate(ops):
                        if o["eng"] != engname:
                            continue
                        for d in o["deps"]:
                            od = ops[d]
                            if od["key"] is not None:
                                sem = ksem[od["key"]]
                                val = 16 * snaps[i][od["key"]]
                                tag = ("k", od["key"])
                            else:
                                if od["eng"] == engname and engname not in self.same_engine_sync:
                                    continue
                                sem = esem[od["eng"]]
                                val = sig_ord[d]
                                tag = ("e", od["eng"])
                            if waited.get(tag, 0) >= val:
                                continue
                            waited[tag] = val
                            eng.wait_ge(sem, val)
                        if o["fn"] == "FENCE":
                            if need_signal[i]:
                                eng.sem_inc(esem[engname], 1)
                            continue
                        ins = o["fn"](eng)
                        if ins is None:
                            if need_signal[i]:
                                eng.sem_inc(esem[engname], 1)
                            continue
                        if o["key"] is not None:
                            ins.then_inc(ksem[o["key"]], 16)
                        elif need_signal[i]:
                            ins.then_inc(esem[engname], 1)
                return body

            block.tensor(make("pe"))
            block.scalar(make("act"))
            block.vector(make("dve"))
            block.gpsimd(make("pool"))
            block.sync(make("sp"))


def build(S, dbg=False, upto=99):
    NB = S // 512
    NO = NB // 2
    S_own = S // 2
    S_all = S
    TB = 256
    NSUB = TB // 128
    NKT = S_all // 128

    nc = bass.Bass("TRN2", target_bir_lowering=False)

    def din(name, shape):
        return nc.dram_tensor(name, list(shape), F32, kind="ExternalInput").ap()

    x_in = din("x_in", [S_all, D])
    g_in = din("g_in", [128, 96])
    wgate = din("w_gate", [4, D, FF])
    wup = din("w_up", [4, D, FF])
    wdown = din("w_down", [4, FF, D])
    w_in = din("w_in", [D, 3072])
    w_out = din("w_out", [D, D])
    lam_in = din("lam_in", [128, 256])
    gsub_in = din("gsub_in", [128, 1])
    w_uv = din("w_uv", [D, 2048])
    buv_col = din("buv_col", [128, 8])
    buv_row = din("buv_row", [128, 1024])
    gln_row = din("gln_row", [128, 1024])
    bln_row = din("bln_row", [128, 1024])
    wspT_in = din("wspT_in", [128, 1024])
    bsp_in = din("bsp_in", [128, 8 * TB])
    w_outc = din("w_outc", [D, D])
    c_ident = din("c_ident", [128, 128])
    c_mats = din("c_mats", [128, 6 * 128])
    c_maskadd = din("c_maskadd", [128, 4 * 512])
    c_maskmul = din("c_maskmul", [128, 4 * 512])
    c_qaug = din("c_qaug", [4, 2, S_own])
    c_bias = din("c_bias", [128, 4 * 68])
    c_neglast = din("c_neglast", [128, 1])
    out = nc.dram_tensor("out", [S_own, D], F32, kind="ExternalOutput").ap()

    kind_s = "ExternalOutput" if dbg else "Internal"

    def dscr(name, shape, dt):
        return nc.dram_tensor(name, list(shape), dt, kind=kind_s).ap()

    x1T = dscr("x1T", [D, S_all], F32)
    kT = dscr("kT", [D, S_all], BF16)
    vS = dscr("vS", [S_all, D], BF16)
    qT = dscr("qT", [D, S_own], BF16)
    oT = dscr("oT", [D, S_own], BF16)
    xa = dscr("xa", [D, S_own], F32)
    xb = dscr("xb", [D, S_own], F32)

    es = ExitStack()
    S_ = Sched(nc)

    NBF = 79872
    NF = 10240
    abf = es.enter_context(nc.sbuf_tensor("abf", [128, NBF], BF16))
    af = es.enter_context(nc.sbuf_tensor("af", [128, NF], F32))
    ident = es.enter_context(nc.sbuf_tensor("ident", [128, 128], F32))
    matsf = es.enter_context(nc.sbuf_tensor("matsf", [128, 6 * 128], F32))
    matsb = es.enter_context(nc.sbuf_tensor("matsb", [128, 6 * 128], BF16))
    gv = es.enter_context(nc.sbuf_tensor("gv", [128, 96], F32))
    gvh = es.enter_context(nc.sbuf_tensor("gvh", [128, 96], F32))
    small = es.enter_context(nc.sbuf_tensor("small", [128, 64], F32))
    ps = [es.enter_context(nc.psum_tensor("ps%d" % i, [128, 512], F32)) for i in range(8)]

    ones_d = matsb[:, 0:128]
    ones_h = matsb[:, 128:256]
    ones_1 = matsb[:, 256:384]
    NU = matsb[:, 384:512]
    negones = matsb[:, 512:640]
    trilmask = matsf[:, 640:768]

    class Arena:
        def __init__(self, t, n):
            self.t, self.n, self.off = t, n, 0

        def reset(self):
            self.off = 0

        def get(self, n):
            o = self.off
            self.off += n
            assert self.off <= self.n, (self.off, self.n)
            return self.t[:, o:o + n]

    A16 = Arena(abf, NBF)
    A32 = Arena(af, NF)

    def DMA(q, out_, in_, reads, writes, key):
        S_.op(q, lambda e: e.dma_start(out=out_, in_=in_), reads, writes, key=key)

    def ACT(out_, in_, func, reads, writes, **kw):
        S_.op("act", lambda e: e.activation(out=out_, in_=in_, func=func, **kw), reads, writes)

    def MM(out_, lhsT, rhs, start, stop, reads, writes, **kw):
        S_.op("pe", lambda e: e.matmul(out_, lhsT=lhsT, rhs=rhs, start=start, stop=stop, **kw), reads, writes)

    def TR(out_, in_, reads, writes):
        S_.op("pe", lambda e: e.transpose(out=out_, in_=in_, identity=ident[:]), list(reads) + ["consts"], writes)

    def TT(eng, out_, in0, in1, op, reads, writes):
        S_.op(eng, lambda e: e.tensor_tensor(out=out_, in0=in0, in1=in1, op=op), reads, writes)

    def STT(eng, out_, in0, scalar, in1, op0, op1, reads, writes):
        S_.op(eng, lambda e: e.scalar_tensor_tensor(out=out_, in0=in0, scalar=scalar, in1=in1, op0=op0, op1=op1), reads, writes)

    def TS(eng, out_, in0, s1, s2, op0, op1, reads, writes):
        if s2 is None:
            S_.op(eng, lambda e: e.tensor_scalar(out=out_, in0=in0, scalar1=s1, scalar2=None, op0=op0), reads, writes)
        else:
            S_.op(eng, lambda e: e.tensor_scalar(out=out_, in0=in0, scalar1=s1, scalar2=s2, op0=op0, op1=op1), reads, writes)

    def CP(eng, out_, in_, reads, writes):
        S_.op(eng, lambda e: e.tensor_copy(out=out_, in_=in_), reads, writes)

    def RCP(out_, in_, reads, writes):
        S_.op("dve", lambda e: e.reciprocal(out=out_, in_=in_), reads, writes)

    DMA("sp", ident[:], c_ident, [], ["consts"], "c0")
    DMA("sp", matsf[:], c_mats, [], ["consts"], "c0")
    DMA("sp", gv[:], g_in, [], ["consts"], "c0")
    CP("dve", matsb[:], matsf[:], ["consts"], ["consts"])
    S_.op("dve", lambda e: e.tensor_scalar(out=gvh[:], in0=gv[:], scalar1=0.5, scalar2=None, op0=ALU.mult), ["consts"], ["consts"])

    def gcol(l, i, c):
        k = (l * 6 + i) * 8 + c
        return gv[:, k:k + 1]

    def gcolh(l, i, c):
        k = (l * 6 + i) * 8 + c
        return gvh[:, k:k + 1]

    def rms_stats(src, nch, ones_t, w, src_res, bufs, tag):
        sq, rt, rstd = bufs
        for c in range(nch):
            sqb = sq[c % 2]
            ACT(sqb[:, 0:w], src(c), AF.Square, [src_res], [("sq", c % 2)])
            MM(ps[6][:, 0:w], ones_t, sqb[:, 0:w], c == 0, c == nch - 1, [("sq", c % 2), "consts"], [("ps", 6)])
        ACT(rt[:, 0:w], ps[6][:, 0:w], AF.Sqrt, [("ps", 6), "consts"], [("rt", tag)], bias=small[:, 0:1], scale=1.0)
        RCP(rstd[:, 0:w], rt[:, 0:w], [("rt", tag)], [("rstd", tag)])
        return rstd

    S_.op("dve", lambda e: e.memset(small[:, 0:1], EPS), [], ["consts"])

    def load_weight(dst, src_ap, nchunk, ncols, name):
        WCH = 512
        for c in range(nchunk):
            for o in range(0, ncols, WCH):
                w_ = min(WCH, ncols - o)
                DMA("pool", dst[:, c * ncols + o:c * ncols + o + w_], src_ap[c * 128:(c + 1) * 128, o:o + w_], [], [("w", name)], ("w", name))

    def ffn_phase(l, j, src, dst, ntok, src_tok=False, dst_tok=False):
        S_.fence()
        A16.reset(); A32.reset()
        Wg = A16.get(KC * FF); Wu = A16.get(KC * FF); Wd = A16.get(FC * D)
        hT = [A16.get(KC * TB) for _ in range(2)]
        aT = A16.get(FC * TB)
        sq = [A16.get(TB) for _ in range(2)]
        sg = [A16.get(TB) for _ in range(2)]
        xT = [A32.get(KC * TB) for _ in range(2)]
        fT = A32.get(KC * TB)
        rt = A32.get(TB); rstd = A32.get(TB); rt2 = A32.get(TB); rstd2 = A32.get(TB)
        xtok = A32.get(NSUB * D) if (src_tok or dst_tok) else None
        wi = l * 2 + j
        load_weight(Wg, wgate[wi], KC, FF, "Wg")
        load_weight(Wu, wup[wi], KC, FF, "Wu")
        load_weight(Wd, wdown[wi], FC, D, "Wd")
        ipre, ipost = (0, 1) if j == 0 else (4, 5)
        nblk = ntok // TB
        misc = [4, 5, 7]
        mi = [0]

        def nextmisc():
            b = misc[mi[0] % 3]
            mi[0] += 1
            return b

        def S1(n):
            sl = n % 2
            xs = xT[sl]
            xres = ("xT", sl)
            if src_tok:
                for s in range(NSUB):
                    DMA("sp", xtok[:, s * D:(s + 1) * D], src[n * TB + s * 128:n * TB + (s + 1) * 128, :], [], ["xtok"], "xtok")
                for c in range(KC):
                    b = nextmisc()
                    for s in range(NSUB):
                        TR(ps[b][:, s * 128:(s + 1) * 128], xtok[:, s * D + c * 128:s * D + (c + 1) * 128], ["xtok"], [("ps", b)])
                    CP("dve", xs[:, c * TB:(c + 1) * TB], ps[b][:, 0:TB], [("ps", b)], [xres])
            else:
                for c in range(KC):
                    DMA("sp", xs[:, c * TB:(c + 1) * TB], src[c * 128:(c + 1) * 128, n * TB:(n + 1) * TB], [], [xres], xres)
            r1 = rms_stats(lambda c: xs[:, c * TB:(c + 1) * TB], KC, ones_d, TB, xres, (sq, rt, rstd), "a")
            for c in range(KC):
                STT("dve", hT[sl][:, c * TB:(c + 1) * TB], xs[:, c * TB:(c + 1) * TB], gcol(l, ipre, c), r1[:, 0:TB],
                    ALU.mult, ALU.mult, [xres, ("rstd", "a"), "consts"], [("hT", sl)])
        def S2(n):
            sl = n % 2
            for f in range(FC):
                bg = f % 2
                bu = 2 + f % 2
                for c in range(KC):
                    MM(ps[bg][:, 0:TB], Wg[:, c * FF + f * 128:c * FF + (f + 1) * 128], hT[sl][:, c * TB:(c + 1) * TB],
                       c == 0, c == KC - 1, [("w", "Wg"), ("hT", sl)], [("ps", bg)])
                for c in range(KC):
                    MM(ps[bu][:, 0:TB], Wu[:, c * FF + f * 128:c * FF + (f + 1) * 128], hT[sl][:, c * TB:(c + 1) * TB],
                       c == 0, c == KC - 1, [("w", "Wu"), ("hT", sl)], [("ps", bu)])
                ACT(sg[f % 2][:, 0:TB], ps[bg][:, 0:TB], AF.Silu, [("ps", bg)], [("sg", f % 2)])
                TT("dve", aT[:, f * TB:(f + 1) * TB], ps[bu][:, 0:TB], sg[f % 2][:, 0:TB], ALU.mult,
                   [("ps", bu), ("sg", f % 2)], [("aT", f)])
        def S3(n):
            for c in range(KC):
                b = nextmisc()
                for f in range(FC):
                    MM(ps[b][:, 0:TB], Wd[:, f * D + c * 128:f * D + (c + 1) * 128], aT[:, f * TB:(f + 1) * TB],
                       f == 0, f == FC - 1, [("w", "Wd"), ("aT", f)], [("ps", b)])
                ACT(fT[:, c * TB:(c + 1) * TB], ps[b][:, 0:TB], AF.Copy, [("ps", b)], ["fT"])
        def S4(n):
            sl = n % 2
            xs = xT[sl]
            xres = ("xT", sl)
            r2 = rms_stats(lambda c: fT[:, c * TB:(c + 1) * TB], KC, ones_d, TB, "fT", (sq, rt2, rstd2), "b")
            for c in range(KC):
                STT("dve", fT[:, c * TB:(c + 1) * TB], fT[:, c * TB:(c + 1) * TB], gcolh(l, ipost, c), r2[:, 0:TB],
                    ALU.mult, ALU.mult, ["fT", ("rstd", "b"), "consts"], ["fT"])
                TT("dve", xs[:, c * TB:(c + 1) * TB], fT[:, c * TB:(c + 1) * TB], xs[:, c * TB:(c + 1) * TB], ALU.add,
                   ["fT", xres], [xres])
            if dst_tok:
                for s in range(NSUB):
                    for hb in range(2):
                        b = nextmisc()
                        for cc in range(4):
                            c = hb * 4 + cc
                            TR(ps[b][:, cc * 128:(cc + 1) * 128], xs[:, c * TB + s * 128:c * TB + (s + 1) * 128], [xres], [("ps", b)])
                        CP("dve", xtok[:, s * D + hb * 512:s * D + (hb + 1) * 512], ps[b][:, 0:512], [("ps", b)], ["xtok"])
                    DMA("sp", dst[n * TB + s * 128:n * TB + (s + 1) * 128, :], xtok[:, s * D:(s + 1) * D], ["xtok"], ["xtok_st"], "xtok_st")
                S_.op("sp", lambda e: None, ["xtok_st"], ["xtok"])
            else:
                for c in range(KC):
                    DMA("sp", dst[c * 128:(c + 1) * 128, n * TB:(n + 1) * TB], xs[:, c * TB:(c + 1) * TB], [xres], [("xst", sl)], ("xst", sl))
                S_.op("sp", lambda e: None, [("xst", sl)], [xres])

        S1(0)
        for n in range(nblk):
            S2(n)
            if n + 1 < nblk:
                S1(n + 1)
            S3(n)
            S4(n)

    def proj_phase():
        S_.fence()
        A16.reset(); A32.reset()
        Win = A16.get(KC * 3072)
        hT = [A16.get(KC * TB) for _ in range(2)]
        sq = [A16.get(TB) for _ in range(2)]
        kst = [A16.get(KC * TB) for _ in range(2)]
        qst = [A16.get(KC * TB) for _ in range(2)]
        vst = [A16.get(NSUB * D) for _ in range(2)]
        xT = [A32.get(KC * TB) for _ in range(2)]
        rt = A32.get(TB); rstd = A32.get(TB)
        load_weight(Win, w_in, KC, 3072, "Win")
        nblk = S_all // TB
        bi = [0]

        def nb():
            b = bi[0] % 6
            bi[0] += 1
            return b

        for n in range(nblk):
            sl = n % 2
            xs = xT[sl]
            xres = ("xT", sl)
            for c in range(KC):
                DMA("sp", xs[:, c * TB:(c + 1) * TB], x1T[c * 128:(c + 1) * 128, n * TB:(n + 1) * TB], [], [xres], xres)
            r1 = rms_stats(lambda c: xs[:, c * TB:(c + 1) * TB], KC, ones_d, TB, xres, (sq, rt, rstd), "a")
            for c in range(KC):
                STT("dve", hT[sl][:, c * TB:(c + 1) * TB], xs[:, c * TB:(c + 1) * TB], gcol(0, 2, c), r1[:, 0:TB],
                    ALU.mult, ALU.mult, [xres, ("rstd", "a"), "consts"], [("hT", sl)])
            for kc in range(8):
                col0 = 512 + 128 * kc if kc < 4 else 2048 + 128 * (kc - 4)
                b = nb()
                for c in range(KC):
                    MM(ps[b][:, 0:TB], Win[:, c * 3072 + col0:c * 3072 + col0 + 128], hT[sl][:, c * TB:(c + 1) * TB],
                       c == 0, c == KC - 1, [("w", "Win"), ("hT", sl)], [("ps", b)])
                ACT(kst[sl][:, kc * TB:(kc + 1) * TB], ps[b][:, 0:TB], AF.Copy, [("ps", b)], [("kst", sl)])
            for kc in range(8):
                DMA("sp", kT[kc * 128:(kc + 1) * 128, n * TB:(n + 1) * TB], kst[sl][:, kc * TB:(kc + 1) * TB], [("kst", sl)], [("kst_st", sl)], ("kst_st", sl))
            S_.op("sp", lambda e: None, [("kst_st", sl)], [("kst", sl)])
            for s in range(NSUB):
                for hv in range(2):
                    col0 = 1024 if hv == 0 else 2560
                    b = nb()
                    for c in range(KC):
                        MM(ps[b][:, 0:512], hT[sl][:, c * TB + s * 128:c * TB + (s + 1) * 128], Win[:, c * 3072 + col0:c * 3072 + col0 + 512],
                           c == 0, c == KC - 1, [("w", "Win"), ("hT", sl)], [("ps", b)])
                    CP("dve", vst[sl][:, s * D + hv * 512:s * D + (hv + 1) * 512], ps[b][:, 0:512], [("ps", b)], [("vst", sl)])
            for s in range(NSUB):
                DMA("sp", vS[n * TB + s * 128:n * TB + (s + 1) * 128, :], vst[sl][:, s * D:(s + 1) * D], [("vst", sl)], [("vst_st", sl)], ("vst_st", sl))
            S_.op("sp", lambda e: None, [("vst_st", sl)], [("vst", sl)])
            if n * TB < S_own:
                for kc in range(8):
                    col0 = 128 * kc if kc < 4 else 1536 + 128 * (kc - 4)
                    b = nb()
                    for c in range(KC):
                        MM(ps[b][:, 0:TB], Win[:, c * 3072 + col0:c * 3072 + col0 + 128], hT[sl][:, c * TB:(c + 1) * TB],
                           c == 0, c == KC - 1, [("w", "Win"), ("hT", sl)], [("ps", b)])
                    ACT(qst[sl][:, kc * TB:(kc + 1) * TB], ps[b][:, 0:TB], AF.Copy, [("ps", b)], [("qst", sl)], scale=0.125)
                for kc in range(8):
                    DMA("sp", qT[kc * 128:(kc + 1) * 128, n * TB:(n + 1) * TB], qst[sl][:, kc * TB:(kc + 1) * TB], [("qst", sl)], [("qst_st", sl)], ("qst_st", sl))
                S_.op("sp", lambda e: None, [("qst_st", sl)], [("qst", sl)])

    def attn_phase():
        S_.fence()
        A16.reset(); A32.reset()
        kt = [A16.get(S_all) for _ in range(2)]
        vt = [A16.get(NKT * 128) for _ in range(2)]
        qt = [A16.get(S_own) for _ in range(2)]
        mmul = A16.get(4 * 512)
        madd = A16.get(4 * 512)
        Eb = [A16.get(512) for _ in range(4)]
        spb = [A16.get(512) for _ in range(3)]
        wb = [A16.get(512) for _ in range(3)]
        Sacc3 = [A16.get(512) for _ in range(3)]
        osb = [A16.get(512) for _ in range(2)]
        sqh = [A16.get(512) for _ in range(2)]
        ef = [A32.get(512) for _ in range(2)]
        tmpf = [A32.get(512) for _ in range(2)]
        rz = A32.get(512)
        tm = [A32.get(512) for _ in range(2)]
        oa = A32.get(512)
        rt = A32.get(512); rstd = A32.get(512)
        biasT = A32.get(4 * 68)
        biasL = A32.get(4 * 68)
        lamt = A32.get(256)
        neglast = small[:, 1:2]
        neglam = small[:, 2:3]
        gsub = small[:, 3:4]
        DMA("pool", madd, c_maskadd, [], ["aconstb"], "aconstb")
        DMA("pool", mmul, c_maskmul, [], ["aconstb"], "aconstb")
        DMA("sp", biasT, c_bias, [], ["aconst"], "aconst")
        DMA("sp", neglast, c_neglast, [], ["aconst"], "aconst")
        DMA("sp", lamt, lam_in, [], ["aconst"], "aconst")
        DMA("sp", gsub, gsub_in, [], ["aconst"], "aconst")
        TS("dve", biasL, biasT, neglast, None, ALU.add, None, ["aconst"], ["aconst2"])
        p1 = tmpf[0]
        TT("dve", p1[:, 0:64], lamt[:, 0:64], lamt[:, 64:128], ALU.mult, ["aconst"], ["lam_p"])
        S_.op("dve", lambda e: e.reduce_sum(out=small[:, 4:5], in_=p1[:, 0:64], axis=AX.X), ["lam_p"], ["lam_s1"])
        TT("dve", p1[:, 64:128], lamt[:, 128:192], lamt[:, 192:256], ALU.mult, ["aconst"], ["lam_p2"])
        S_.op("dve", lambda e: e.reduce_sum(out=small[:, 5:6], in_=p1[:, 64:128], axis=AX.X), ["lam_p2"], ["lam_s2"])
        ACT(small[:, 6:8], small[:, 4:6], AF.Exp, ["lam_s1", "lam_s2"], ["lam_e"])
        TT("dve", small[:, 8:9], small[:, 7:8], small[:, 6:7], ALU.subtract, ["lam_e"], ["lam_d"])
        TS("dve", neglam, small[:, 8:9], -LAMBDA_INIT0, None, ALU.add, None, ["lam_d"], ["aconst3"])
        TS("dve", small[:, 9:10], gsub, 1.0 - LAMBDA_INIT0, None, ALU.mult, None, ["aconst"], ["aconst4"])
        gsubs = small[:, 9:10]
        for i in range(2):
            S_.op("dve", lambda e, i=i: e.memset(kt[i][64:66, :], 1.0), [], [("kt", i)])

        def tiles(i):
            seq = []
            for ib in range(i, -1, -1):
                for r in range(3, -1, -1):
                    seq.append((ib * 512 + r * 128, 1024 * (i - ib) - 128 * r, "diag" if ib == i else "full", r))
                for r in range(3, -1, -1):
                    seq.append((S_own + ib * 512 + r * 128, 1024 * (i - ib) + 512 - 128 * r, "last" if ib == 0 else "full", r))
            return seq

        cnt = {"sc": 0, "E": 0, "acc": 0, "z": 0, "e": 0, "sp": 0, "w": 0, "tmp": 0, "osb": 0}

        def rot(name, n):
            v = cnt[name] % n
            cnt[name] += 1
            return v

        tm0_all = A32.get(S_own)
        ui = 0
        for h in range(4):
            vsl = h % 2
            for t in range(NKT):
                DMA("sp", vt[vsl][:, t * 128:(t + 1) * 128], vS[t * 128:(t + 1) * 128, h * 128:(h + 1) * 128],
                    [], [("vt", vsl)], ("vt", vsl))
            for m in range(2):
                u = 2 * h + m
                ksl = ui % 2
                ui += 1
                for o_ in range(0, S_all, 2048):
                    DMA("sp", kt[ksl][0:64, o_:min(o_ + 2048, S_all)], kT[u * 64:(u + 1) * 64, o_:min(o_ + 2048, S_all)], [], [("kt", ksl)], ("kt", ksl))
                for o_ in range(0, S_own, 2048):
                    DMA("sp", qt[ksl][0:64, o_:min(o_ + 2048, S_own)], qT[u * 64:(u + 1) * 64, o_:min(o_ + 2048, S_own)], [], [("qt", ksl)], ("qt", ksl))
                for o_ in range(0, S_own, 512):
                    DMA("pool", qt[ksl][64:66, o_:o_ + 512], c_qaug[h][:, o_:o_ + 512], [], [("qt", ksl)], ("qta", ksl))
                for i in range(NO):
                    a = rot("acc", 2)
                    bO, bZ = 4 + a * 2, 5 + a * 2
                    seq = tiles(i)
                    pend = []

                    def pv(last_):
                        eb0, koff0, ti0 = pend.pop(0)
                        MM(ps[bO][:, :], vt[vsl][:, koff0:koff0 + 128], Eb[eb0], ti0 == 0, last_, [("vt", vsl), ("E", eb0)], [("ps", bO)])
                        MM(ps[bZ][:, :], ones_1, Eb[eb0], ti0 == 0, last_, ["consts", ("E", eb0)], [("ps", bZ)])

                    for ti, (koff, d0, kind, r) in enumerate(seq):
                        bs = rot("sc", 3)
                        MM(ps[bs][:, :], kt[ksl][0:66, koff:koff + 128], qt[ksl][0:66, i * 512:(i + 1) * 512], True, True,
                           [("kt", ksl), ("qt", ksl)], [("ps", bs)])
                        eb = rot("E", 4)
                        idx = d0 // 128 + 3
                        bcol = h * 68 + idx
                        if kind == "diag":
                            tb_ = rot("tmp", 2)
                            TT("dve", tmpf[tb_], ps[bs][:, :], madd[:, r * 512:(r + 1) * 512], ALU.add, [("ps", bs), "aconstb"], [("tmpf", tb_)])
                            ACT(Eb[eb], tmpf[tb_], AF.Exp, [("tmpf", tb_), "aconst"], [("E", eb)], bias=biasT[:, bcol:bcol + 1], scale=1.0)
                        elif kind == "last":
                            ACT(Eb[eb], ps[bs][:, :], AF.Exp, [("ps", bs), "aconst2"], [("E", eb)], bias=biasL[:, bcol:bcol + 1], scale=1.0)
                        else:
                            ACT(Eb[eb], ps[bs][:, :], AF.Exp, [("ps", bs), "aconst"], [("E", eb)], bias=biasT[:, bcol:bcol + 1], scale=1.0)
                        pend.append((eb, koff, ti))
                        if len(pend) > 2:
                            pv(False)
                    while len(pend) > 1:
                        pv(False)
                    pv(True)
                    RCP(rz, ps[bZ][:, :], [("ps", bZ)], ["rz"])
                    if m == 0:
                        TT("dve", tm0_all[:, i * 512:(i + 1) * 512], ps[bO][:, :], rz, ALU.mult, [("ps", bO), "rz"], [("tm0", i)])
                    else:
                        TT("dve", tm[0], ps[bO][:, :], rz, ALU.mult, [("ps", bO), "rz"], ["tm1"])
                        STT("dve", oa, tm[0], neglam, tm0_all[:, i * 512:(i + 1) * 512], ALU.mult, ALU.add,
                            ["tm1", ("tm0", i), "aconst3"], ["oa"])
                        ACT(sqh[0], oa, AF.Square, ["oa"], ["sqh"])
                        MM(ps[3][:, :], ones_h, sqh[0], True, True, ["sqh", "consts"], [("ps", 3)])
                        ACT(rt, ps[3][:, :], AF.Sqrt, [("ps", 3)], ["rt_h"], bias=small[:, 0:1], scale=1.0)
                        RCP(rstd, rt, ["rt_h"], ["rstd_h"])
                        ob_ = rot("osb", 2)
                        STT("dve", osb[ob_], oa, gsubs, rstd, ALU.mult, ALU.mult, ["oa", "rstd_h", "aconst4"], [("osb", ob_)])
                        DMA("sp", oT[h * 128:(h + 1) * 128, i * 512:(i + 1) * 512], osb[ob_], [("osb", ob_)], [("osb_st", ob_)], ("osb_st", ob_))
                        S_.op("sp", lambda e: None, [("osb_st", ob_)], [("osb", ob_)])

        Sac = Sacc3
        for h in range(8):
            vsl = h % 2
            ksl = h % 2
            u = 8 + h
            for t in range(NKT):
                DMA("sp", vt[vsl][:, t * 64:(t + 1) * 64], vS[t * 128:(t + 1) * 128, 512 + h * 64:512 + (h + 1) * 64],
                    [], [("vt", vsl)], ("vt", vsl))
            for o_ in range(0, S_all, 2048):
                DMA("sp", kt[ksl][0:64, o_:min(o_ + 2048, S_all)], kT[u * 64:(u + 1) * 64, o_:min(o_ + 2048, S_all)], [], [("kt", ksl)], ("kt", ksl))
            for o_ in range(0, S_own, 2048):
                DMA("sp", qt[ksl][0:64, o_:min(o_ + 2048, S_own)], qT[u * 64:(u + 1) * 64, o_:min(o_ + 2048, S_own)], [], [("qt", ksl)], ("qt", ksl))
            for i in range(NO):
                a = rot("acc", 2)
                bo = 4 + a
                seq = tiles(i)
                jobs = []
                for ti, (koff, d0, kind, r) in enumerate(seq):
                    jobs.append(dict(ti=ti, koff=koff, kind=kind, r=r))

                def stA(j):
                    bs = rot("z", 4)
                    j["bs"] = bs
                    MM(ps[bs][:, :], kt[ksl][0:64, j["koff"]:j["koff"] + 128], qt[ksl][0:64, i * 512:(i + 1) * 512], True, True,
                       [("kt", ksl), ("qt", ksl)], [("ps", bs)])
                    eb = rot("e", 2)
                    sb_ = rot("sp", 3)
                    j["sp"] = sb_
                    if j["kind"] == "last":
                        ACT(ef[eb], ps[bs][:, :], AF.Exp, [("ps", bs), "aconst"], [("ef", eb)], bias=neglast, scale=1.0)
                    else:
                        ACT(ef[eb], ps[bs][:, :], AF.Exp, [("ps", bs)], [("ef", eb)])
                    j["eb"] = eb

                def stA2(j):
                    eb = j["eb"]
                    sb_ = j["sp"]
                    ACT(spb[sb_], ef[eb], AF.Ln, [("ef", eb)], [("spb", sb_)], bias=1.0, scale=1.0)
                    if j["kind"] == "diag":
                        TT("dve", spb[sb_], spb[sb_], mmul[:, j["r"] * 512:(j["r"] + 1) * 512], ALU.mult, [("spb", sb_), "aconstb"], [("spb", sb_)])
                    sa = j["ti"] % 3
                    j["sa"] = sa
                    if j["ti"] == 0:
                        CP("pool", Sac[sa], spb[sb_], [("spb", sb_)], [("Sac", sa)])
                    else:
                        TT("pool", Sac[sa], Sac[(j["ti"] - 1) % 3], spb[sb_], ALU.add, [("Sac", (j["ti"] - 1) % 3), ("spb", sb_)], [("Sac", sa)])

                def stB(j):
                    bs = j["bs"]
                    sb_ = j["sp"]
                    MM(ps[bs][:, :], NU, spb[sb_], False, j["ti"] == 0, ["consts", ("spb", sb_)], [("ps", bs)], skip_group_check=True)
                    if j["ti"] > 0:
                        pa = (j["ti"] - 1) % 3
                        MM(ps[bs][:, :], negones, Sac[pa], False, True, ["consts", ("Sac", pa)], [("ps", bs)], skip_group_check=True)
                    wi_ = rot("w", 3)
                    j["w"] = wi_
                    if j["kind"] == "last":
                        ACT(wb[wi_], ps[bs][:, :], AF.Exp, [("ps", bs), "aconst"], [("wb", wi_)], bias=neglast, scale=1.0)
                    else:
                        ACT(wb[wi_], ps[bs][:, :], AF.Exp, [("ps", bs)], [("wb", wi_)])
                    if j["kind"] == "diag":
                        TT("dve", wb[wi_], wb[wi_], mmul[:, j["r"] * 512:(j["r"] + 1) * 512], ALU.mult, [("wb", wi_), "aconstb"], [("wb", wi_)])

                def stC(j):
                    t = j["koff"] // 128
                    MM(ps[bo][0:64, :], vt[vsl][:, t * 64:(t + 1) * 64], wb[j["w"]], j["ti"] == 0, j["ti"] == len(jobs) - 1,
                       [("vt", vsl), ("wb", j["w"])], [("ps", bo)])

                N = len(jobs)
                for n in range(N + 2):
                    if n < N:
                        stA(jobs[n])
                    if 0 <= n - 1 < N:
                        stB(jobs[n - 1])
                    if n < N:
                        stA2(jobs[n])
                    if 0 <= n - 2 < N:
                        stC(jobs[n - 2])
                ob_ = rot("osb", 2)
                ACT(osb[ob_][0:64, :], ps[bo][0:64, :], AF.Copy, [("ps", bo)], [("osb", ob_)])
                DMA("sp", oT[512 + h * 64:512 + (h + 1) * 64, i * 512:(i + 1) * 512], osb[ob_][0:64, :], [("osb", ob_)], [("osb_st", ob_)], ("osb_st", ob_))
                S_.op("sp", lambda e: None, [("osb_st", ob_)], [("osb", ob_)])

    def outproj_phase():
        S_.fence()
        A16.reset(); A32.reset()
        Wo = A16.get(KC * D)
        oTt = [A16.get(KC * TB) for _ in range(2)]
        sq = [A16.get(TB) for _ in range(2)]
        xT = [A32.get(KC * TB) for _ in range(2)]
        fT = A32.get(KC * TB)
        rt = A32.get(TB); rstd = A32.get(TB)
        load_weight(Wo, w_out, KC, D, "Wo")
        bi = [0]
        for n in range(S_own // TB):
            sl = n % 2
            xs = xT[sl]
            xres = ("xT", sl)
            for c in range(KC):
                DMA("sp", xs[:, c * TB:(c + 1) * TB], x1T[c * 128:(c + 1) * 128, n * TB:(n + 1) * TB], [], [xres], xres)
                DMA("sp", oTt[sl][:, c * TB:(c + 1) * TB], oT[c * 128:(c + 1) * 128, n * TB:(n + 1) * TB], [], [("oTt", sl)], ("oTt", sl))
            for c2 in range(KC):
                b = bi[0] % 4
                bi[0] += 1
                for c in range(KC):
                    MM(ps[b][:, 0:TB], Wo[:, c * D + c2 * 128:c * D + (c2 + 1) * 128], oTt[sl][:, c * TB:(c + 1) * TB],
                       c == 0, c == KC - 1, [("w", "Wo"), ("oTt", sl)], [("ps", b)])
                ACT(fT[:, c2 * TB:(c2 + 1) * TB], ps[b][:, 0:TB], AF.Copy, [("ps", b)], ["fT"])
            r2 = rms_stats(lambda c: fT[:, c * TB:(c + 1) * TB], KC, ones_d, TB, "fT", (sq, rt, rstd), "b")
            for c in range(KC):
                STT("dve", fT[:, c * TB:(c + 1) * TB], fT[:, c * TB:(c + 1) * TB], gcol(0, 3, c), r2[:, 0:TB],
                    ALU.mult, ALU.mult, ["fT", ("rstd", "b"), "consts"], ["fT"])
                TT("dve", xs[:, c * TB:(c + 1) * TB], fT[:, c * TB:(c + 1) * TB], xs[:, c * TB:(c + 1) * TB], ALU.add,
                   ["fT", xres], [xres])
            for c in range(KC):
                DMA("sp", xa[c * 128:(c + 1) * 128, n * TB:(n + 1) * TB], xs[:, c * TB:(c + 1) * TB], [xres], [("xst", sl)], ("xst", sl))
            S_.op("sp", lambda e: None, [("xst", sl)], [xres])

    def sgu_phase(src, dst):
        S_.fence()
        A16.reset(); A32.reset()
        Wuv = A16.get(KC * 2048)
        Woc = A16.get(KC * D)
        wspb = A16.get(1024)
        hT = [A16.get(KC * TB) for _ in range(2)]
        sq = [A16.get(TB) for _ in range(2)]
        vnb = [A16.get(D) for _ in range(NSUB)]
        gated = A16.get(KC * TB)
        junk = A16.get(D)
        xT = [A32.get(KC * TB) for _ in range(2)]
        fT = A32.get(KC * TB)
        A32b = Arena(abf.bitcast(F32), NBF // 2)
        A32b.off = 20000
        uT = A32b.get(KC * TB)
        vtok = [A32b.get(D) for _ in range(NSUB)]
        rt = A32.get(TB); rstd = A32.get(TB); rt2 = A32.get(TB); rstd2 = A32.get(TB)
        wspf = fT
        tmpg = A32.get(TB)
        load_weight(Wuv, w_uv, KC, 2048, "Wuv")
        load_weight(Woc, w_outc, KC, D, "Woc")
        prm = A32b.get(3 * D + 8 * TB + 8)
        buvr = prm[:, 0:D]; glnr = prm[:, D:2 * D]; blnr = prm[:, 2 * D:3 * D]
        bspt = prm[:, 3 * D:3 * D + 8 * TB]
        buvc = prm[:, 3 * D + 8 * TB:3 * D + 8 * TB + 8]
        DMA("sp", buvr, buv_row, [], ["sconst"], "sconst")
        DMA("sp", glnr, gln_row, [], ["sconst"], "sconst")
        DMA("sp", blnr, bln_row, [], ["sconst"], "sconst")
        DMA("sp", bspt, bsp_in, [], ["sconst"], "sconst")
        DMA("sp", buvc, buv_col, [], ["sconst"], "sconst")
        DMA("sp", wspf[:, 0:1024], wspT_in, [], ["wspf"], "wspf")
        for g in range(8):
            TT("dve", wspb[:, g * 128:(g + 1) * 128], wspf[:, g * 128:(g + 1) * 128], trilmask, ALU.mult, ["wspf", "consts"], ["wspb"])
        S_.op("sp", lambda e: None, ["wspb"], ["fT"])
        sm = A32b.get(16)
        bi = [0]

        def nb():
            b = bi[0] % 6
            bi[0] += 1
            return b

        for n in range(S_own // TB):
            sl = n % 2
            xs = xT[sl]
            xres = ("xT", sl)
            for c in range(KC):
                DMA("sp", xs[:, c * TB:(c + 1) * TB], src[c * 128:(c + 1) * 128, n * TB:(n + 1) * TB], [], [xres], xres)
            r1 = rms_stats(lambda c: xs[:, c * TB:(c + 1) * TB], KC, ones_d, TB, xres, (sq, rt, rstd), "a")
            for c in range(KC):
                STT("dve", hT[sl][:, c * TB:(c + 1) * TB], xs[:, c * TB:(c + 1) * TB], gcol(1, 2, c), r1[:, 0:TB],
                    ALU.mult, ALU.mult, [xres, ("rstd", "a"), "consts"], [("hT", sl)])
            for fc in range(8):
                b = nb()
                for c in range(KC):
                    MM(ps[b][:, 0:TB], Wuv[:, c * 2048 + fc * 128:c * 2048 + (fc + 1) * 128], hT[sl][:, c * TB:(c + 1) * TB],
                       c == 0, c == KC - 1, [("w", "Wuv"), ("hT", sl)], [("ps", b)])
                ACT(uT[:, fc * TB:(fc + 1) * TB], ps[b][:, 0:TB], AF.Gelu_apprx_tanh, [("ps", b), "sconst"], [("uT", fc)],
                    bias=buvc[:, fc:fc + 1], scale=1.0)
            for s in range(NSUB):
                for hv in range(2):
                    b = nb()
                    for c in range(KC):
                        MM(ps[b][:, 0:512], hT[sl][:, c * TB + s * 128:c * TB + (s + 1) * 128],
                           Wuv[:, c * 2048 + 1024 + hv * 512:c * 2048 + 1024 + (hv + 1) * 512],
                           c == 0, c == KC - 1, [("w", "Wuv"), ("hT", sl)], [("ps", b)])
                    TT("dve", vtok[s][:, hv * 512:(hv + 1) * 512], ps[b][:, 0:512], buvr[:, hv * 512:(hv + 1) * 512], ALU.add,
                       [("ps", b), "sconst"], [("vtok", s)])
                ACT(vtok[s], vtok[s], AF.Gelu_apprx_tanh, [("vtok", s)], [("vtok", s)])
                S_.op("dve", lambda e, s=s: e.reduce_sum(out=sm[:, 0:1], in_=vtok[s], axis=AX.X), [("vtok", s)], ["sm0"])
                ACT(junk, vtok[s], AF.Square, [("vtok", s)], ["junk", "sm1"], accum_out=sm[:, 1:2])
                TS("dve", sm[:, 2:3], sm[:, 0:1], 1.0 / 1024, None, ALU.mult, None, ["sm0"], ["sm2"])
                TT("dve", sm[:, 3:4], sm[:, 2:3], sm[:, 2:3], ALU.mult, ["sm2"], ["sm3"])
                STT("dve", sm[:, 4:5], sm[:, 1:2], 1.0 / 1024, sm[:, 3:4], ALU.mult, ALU.subtract, ["sm1", "sm3"], ["sm4"])
                ACT(sm[:, 5:6], sm[:, 4:5], AF.Sqrt, ["sm4"], ["sm5"], bias=small[:, 0:1], scale=1.0)
                RCP(sm[:, 6:7], sm[:, 5:6], ["sm5"], ["sm6"])
                TS("dve", vtok[s], vtok[s], sm[:, 2:3], sm[:, 6:7], ALU.subtract, ALU.mult, [("vtok", s), "sm2", "sm6"], [("vtok", s)])
                TT("dve", vtok[s], vtok[s], glnr, ALU.mult, [("vtok", s), "sconst"], [("vtok", s)])
                TT("dve", vnb[s], vtok[s], blnr, ALU.add, [("vtok", s), "sconst"], [("vnb", s)])
            for g in range(8):
                b = nb()
                for s in range(NSUB):
                    MM(ps[b][:, s * 128:(s + 1) * 128], vnb[s][:, g * 128:(g + 1) * 128], wspb[:, g * 128:(g + 1) * 128], True, True,
                       [("vnb", s), "wspb"], [("ps", b)])
                TT("dve", tmpg, ps[b][:, 0:TB], bspt[:, g * TB:(g + 1) * TB], ALU.add, [("ps", b), "sconst"], ["tmpg"])
                TT("dve", gated[:, g * TB:(g + 1) * TB], tmpg, uT[:, g * TB:(g + 1) * TB], ALU.mult, ["tmpg", ("uT", g)], [("gated", g)])
            for c2 in range(KC):
                b = nb()
                for g in range(8):
                    MM(ps[b][:, 0:TB], Woc[:, g * D + c2 * 128:g * D + (c2 + 1) * 128], gated[:, g * TB:(g + 1) * TB],
                       g == 0, g == 7, [("w", "Woc"), ("gated", g)], [("ps", b)])
                ACT(fT[:, c2 * TB:(c2 + 1) * TB], ps[b][:, 0:TB], AF.Copy, [("ps", b)], ["fT"])
            r2 = rms_stats(lambda c: fT[:, c * TB:(c + 1) * TB], KC, ones_d, TB, "fT", (sq, rt2, rstd2), "b")
            for c in range(KC):
                STT("dve", fT[:, c * TB:(c + 1) * TB], fT[:, c * TB:(c + 1) * TB], gcol(1, 3, c), r2[:, 0:TB],
                    ALU.mult, ALU.mult, ["fT", ("rstd", "b"), "consts"], ["fT"])
                TT("dve", xs[:, c * TB:(c + 1) * TB], fT[:, c * TB:(c + 1) * TB], xs[:, c * TB:(c + 1) * TB], ALU.add,
                   ["fT", xres], [xres])
            for c in range(KC):
                DMA("sp", dst[c * 128:(c + 1) * 128, n * TB:(n + 1) * TB], xs[:, c * TB:(c + 1) * TB], [xres], [("xst", sl)], ("xst", sl))
            S_.op("sp", lambda e: None, [("xst", sl)], [xres])

    if upto >= 1:
        ffn_phase(0, 0, x_in, x1T, S_all, src_tok=True)
    if upto >= 2:
        proj_phase()
    if upto >= 3:
        attn_phase()
    if upto >= 4:
        outproj_phase()
    if upto >= 5:
        ffn_phase(0, 1, xa, xb, S_own)
    if upto >= 6:
        ffn_phase(1, 0, xb, xa, S_own)
    if upto >= 7:
        sgu_phase(xa, xb)
    if upto >= 8:
        ffn_phase(1, 1, xb, out, S_own, dst_tok=True)
    S_.fence()
    S_.emit()
    es.close()
    return nc


def host_consts(S):
    S_own = S // 2
    TB = 256
    p = np.arange(128)
    mats = np.zeros((128, 6, 128), np.float32)
    mats[:, 0] = 1.0 / 1024
    mats[:, 1] = 1.0 / 128
    mats[:, 2] = 1.0
    mats[:, 3] = -(p[:, None] >= p[None, :]).astype(np.float32)
    mats[:, 4] = -1.0
    mats[:, 5] = (p[:, None] <= p[None, :]).astype(np.float32)
    t = np.arange(512)
    maskadd = np.zeros((128, 4, 512), np.float32)
    maskmul = np.zeros((128, 4, 512), np.float32)
    for r in range(4):
        s_ = 128 * r + p
        maskadd[:, r] = np.where(s_[:, None] <= t[None, :], 0.0, NEG)
        maskmul[:, r] = (s_[:, None] < t[None, :]).astype(np.float32)
    qaug = np.zeros((4, 2, S_own), np.float32)
    tt = np.arange(S_own) % 512
    for h in range(4):
        qaug[h, 0] = -SLOPES[h] * (tt // 16 * 16)
        qaug[h, 1] = -SLOPES[h] * (tt % 16)
    bias = np.zeros((128, 4, 68), np.float32)
    for h in range(4):
        for idx in range(68):
            bias[:, h, idx] = SLOPES[h] * (p - 128.0 * (idx - 3))
    return dict(
        c_ident=np.eye(128, dtype=np.float32),
        c_mats=mats.reshape(128, -1),
        c_maskadd=maskadd.reshape(128, -1),
        c_maskmul=maskmul.reshape(128, -1),
        c_qaug=qaug,
        c_bias=bias.reshape(128, -1),
    )


def make_in_maps(S, x, g_norm, w_ffn_gate, w_ffn_up, w_ffn_down, w_in_ab, w_out_ab, lambda_params, g_subln,
                 w_uv, b_uv, g_sgu_ln, b_sgu_ln, w_spatial, b_spatial, w_out_c):
    B = x.shape[0]
    NB = S // 512
    NO = NB // 2
    TB = 256
    f = lambda a: np.ascontiguousarray(np.asarray(a, dtype=np.float32))
    cst = host_consts(S)
    shared = dict(
        g_in=f(np.asarray(g_norm).reshape(2, 6, 8, 128).transpose(3, 0, 1, 2).reshape(128, 96)),
        w_gate=f(np.asarray(w_ffn_gate).reshape(4, D, FF)),
        w_up=f(np.asarray(w_ffn_up).reshape(4, D, FF)),
        w_down=f(np.asarray(w_ffn_down).reshape(4, FF, D)),
        w_in=f(np.asarray(w_in_ab)[0]),
        w_out=f(np.asarray(w_out_ab)[0]),
        lam_in=f(np.broadcast_to(np.asarray(lambda_params)[0].reshape(1, 256), (128, 256))),
        gsub_in=f(np.asarray(g_subln)[0].reshape(128, 1)),
        w_uv=f(np.asarray(w_uv)[0]),
        buv_col=f(np.asarray(b_uv)[0][:1024].reshape(8, 128).T),
        buv_row=f(np.broadcast_to(np.asarray(b_uv)[0][1024:].reshape(1, 1024), (128, 1024))),
        gln_row=f(np.broadcast_to(np.asarray(g_sgu_ln)[0].reshape(1, 1024), (128, 1024))),
        bln_row=f(np.broadcast_to(np.asarray(b_sgu_ln)[0].reshape(1, 1024), (128, 1024))),
        wspT_in=f(np.asarray(w_spatial)[0].transpose(2, 0, 1).reshape(128, 1024)),
        bsp_in=f(np.broadcast_to(np.tile(np.asarray(b_spatial)[0], (1, TB // 128)).reshape(1, 8 * TB), (128, 8 * TB))),
        w_outc=f(np.asarray(w_out_c)[0]),
        **cst,
    )
    x = np.asarray(x, dtype=np.float32)
    in_maps = []
    for b in range(B):
        xb_ = x[b].reshape(NB, 512, D)
        for p in range(2):
            own = [2 * i + p for i in range(NO)]
            if p == 1:
                other = [2 * i for i in range(NO)]
            else:
                other = [NB - 1] + [2 * i - 1 for i in range(1, NO)]
            xp = np.ascontiguousarray(xb_[own + other].reshape(S, D))
            m = dict(shared)
            m["x_in"] = xp
            m["c_neglast"] = np.full((128, 1), 0.0 if p == 1 else NEG, np.float32)
            in_maps.append(m)
    return in_maps


_CACHE = {}


def run(S, inputs, dbg=False, upto=99):
    key = (S, dbg, upto)
    nc = build(S, dbg=dbg, upto=upto)
    in_maps = make_in_maps(S, **inputs)
    res = run_bass_kernel_spmd(nc, in_maps, core_ids=list(range(len(in_maps))))
    return res


def kernel(**inputs):
    x = np.asarray(inputs["x"])
    B, S, _ = x.shape
    res = run(S, inputs)
    NB = S // 512
    NO = NB // 2
    outp = np.zeros((B, S, D), np.float32)
    k = 0
    for b in range(B):
        ob = outp[b].reshape(NB, 512, D)
        for p in range(2):
            r = np.asarray(res.results[k]["out"]).reshape(NO, 512, D)
            k += 1
            for i in range(NO):
                ob[2 * i + p] = r[i]
    return outp
```

```python
import numpy as np
import concourse.bass as bass
import concourse.mybir as mybir
from concourse.bass_utils import run_bass_kernel_spmd
from contextlib import ExitStack

F32 = mybir.dt.float32
BF16 = mybir.dt.bfloat16
AF = mybir.ActivationFunctionType
ALU = mybir.AluOpType
AX = mybir.AxisListType

D = 1024
KC = 8
FF = 2816
FC = 22
EPS = 1e-6
NEG = -30000.0
SLOPES = [2.0 ** (-8.0 * (i + 1) / 4) for i in range(4)]
LAMBDA_INIT0 = 0.8 - 0.6 * float(np.exp(-0.3 * 0))
ENGS = ("pe", "act", "dve", "pool", "sp")


class Sched:
    def __init__(self, nc, same_engine_sync=("act", "dve", "pool")):
        self.nc = nc
        self.ops = []
        self.last_w = {}
        self.readers = {}
        self.same_engine_sync = set(same_engine_sync)
        self.fence_idx = None

    def op(self, eng, fn, reads=(), writes=(), key=None):
        i = len(self.ops)
        deps = set()
        for r in reads:
            w = self.last_w.get(r)
            if w is not None:
                deps.add(w)
        for w_ in writes:
            w = self.last_w.get(w_)
            if w is not None:
                deps.add(w)
            deps.update(self.readers.get(w_, ()))
        for r in reads:
            self.readers.setdefault(r, []).append(i)
        for w_ in writes:
            self.last_w[w_] = i
            self.readers[w_] = []
        if self.fence_idx is not None:
            deps.add(self.fence_idx)
        deps.discard(i)
        self.ops.append(dict(eng=eng, fn=fn, deps=sorted(deps), key=key))
        return i

    def fence(self):
        allres = set(self.last_w.keys()) | set(self.readers.keys())
        allres.add("__fence__")
        self.fence_idx = self.op("sp", "FENCE", reads=(), writes=sorted(allres, key=str))

    def emit(self):
        nc = self.nc
        ops = self.ops
        need_signal = [False] * len(ops)
        key_cnt = {}
        snaps = []
        for i, o in enumerate(ops):
            snap = {}
            o["deps"] = [d for d in o["deps"] if not (o["key"] is not None and ops[d]["key"] == o["key"])]
            for d in o["deps"]:
                od = ops[d]
                if od["key"] is not None:
                    snap[od["key"]] = key_cnt[od["key"]]
                else:
                    if od["eng"] != o["eng"] or od["eng"] in self.same_engine_sync:
                        need_signal[d] = True
            snaps.append(snap)
            if o["key"] is not None:
                key_cnt[o["key"]] = key_cnt.get(o["key"], 0) + 1
        sig_ord = [0] * len(ops)
        cnt = {e: 0 for e in ENGS}
        for i, o in enumerate(ops):
            if o["key"] is None and need_signal[i]:
                cnt[o["eng"]] += 1
                sig_ord[i] = cnt[o["eng"]]
        keys = sorted({o["key"] for o in ops if o["key"] is not None}, key=str)
        self.n_sems = len(keys) + len(ENGS)
        with ExitStack() as es:
            esem = {e: es.enter_context(nc.semaphore("e_" + e)) for e in ENGS}
            ksem = {k: es.enter_context(nc.semaphore("k_%d" % n)) for n, k in enumerate(keys)}
            block = es.enter_context(nc.Block())

            def make(engname):
                def body(eng):
                    waited = {}
                    for i, o in enumerate(ops):
                        if o["eng"] != engname:
                            continue
                        for d in o["deps"]:
                            od = ops[d]
                            if od["key"] is not None:
                                sem = ksem[od["key"]]
                                val = 16 * snaps[i][od["key"]]
                                tag = ("k", od["key"])
                            else:
                                if od["eng"] == engname and engname not in self.same_engine_sync:
                                    continue
                                sem = esem[od["eng"]]
                                val = sig_ord[d]
                                tag = ("e", od["eng"])
                            if waited.get(tag, 0) >= val:
                                continue
                            waited[tag] = val
                            eng.wait_ge(sem, val)
                        if o["fn"] == "FENCE":
                            if need_signal[i]:
                                eng.sem_inc(esem[engname], 1)
                            continue
                        ins = o["fn"](eng)
                        if ins is None:
                            if need_signal[i]:
                                eng.sem_inc(esem[engname], 1)
                            continue
                        if o["key"] is not None:
                            ins.then_inc(ksem[o["key"]], 16)
                        elif need_signal[i]:
                            ins.then_inc(esem[engname], 1)
                return body

            block.tensor(make("pe"))
            block.scalar(make("act"))
            block.vector(make("dve"))
            block.gpsimd(make("pool"))
            block.sync(make("sp"))


def build(S, dbg=False, upto=99):
    NB = S // 512
    NO = NB // 2
    S_own = S // 2
    S_all = S
    TB = 256
    NSUB = TB // 128
    NKT = S_all // 128

    nc = bass.Bass("TRN2", target_bir_lowering=False)

    def din(name, shape):
        return nc.dram_tensor(name, list(shape), F32, kind="ExternalInput").ap()

    x_in = din("x_in", [S_all, D])
    g_in = din("g_in", [128, 96])
    wgate = din("w_gate", [4, D, FF])
    wup = din("w_up", [4, D, FF])
    wdown = din("w_down", [4, FF, D])
    w_in = din("w_in", [D, 3072])
    w_out = din("w_out", [D, D])
    lam_in = din("lam_in", [128, 256])
    gsub_in = din("gsub_in", [128, 1])
    w_uv = din("w_uv", [D, 2048])
    buv_col = din("buv_col", [128, 8])
    buv_row = din("buv_row", [128, 1024])
    gln_row = din("gln_row", [128, 1024])
    bln_row = din("bln_row", [128, 1024])
    wspT_in = din("wspT_in", [128, 1024])
    bsp_in = din("bsp_in", [128, 8 * TB])
    w_outc = din("w_outc", [D, D])
    c_ident = din("c_ident", [128, 128])
    c_mats = din("c_mats", [128, 6 * 128])
    c_maskadd = din("c_maskadd", [128, 4 * 512])
    c_maskmul = din("c_maskmul", [128, 4 * 512])
    c_qaug = din("c_qaug", [4, 2, S_own])
    c_bias = din("c_bias", [128, 4 * 68])
    c_neglast = din("c_neglast", [128, 1])
    out = nc.dram_tensor("out", [S_own, D], F32, kind="ExternalOutput").ap()

    kind_s = "ExternalOutput" if dbg else "Internal"

    def dscr(name, shape, dt):
        return nc.dram_tensor(name, list(shape), dt, kind=kind_s).ap()

    x1T = dscr("x1T", [D, S_all], F32)
    kT = dscr("kT", [D, S_all], BF16)
    vS = dscr("vS", [S_all, D], BF16)
    qT = dscr("qT", [D, S_own], BF16)
    oT = dscr("oT", [D, S_own], BF16)
    xa = dscr("xa", [D, S_own], F32)
    xb = dscr("xb", [D, S_own], F32)

    es = ExitStack()
    S_ = Sched(nc)

    NBF = 79872
    NF = 10240
    abf = es.enter_context(nc.sbuf_tensor("abf", [128, NBF], BF16))
    af = es.enter_context(nc.sbuf_tensor("af", [128, NF], F32))
    ident = es.enter_context(nc.sbuf_tensor("ident", [128, 128], F32))
    matsf = es.enter_context(nc.sbuf_tensor("matsf", [128, 6 * 128], F32))
    matsb = es.enter_context(nc.sbuf_tensor("matsb", [128, 6 * 128], BF16))
    gv = es.enter_context(nc.sbuf_tensor("gv", [128, 96], F32))
    gvh = es.enter_context(nc.sbuf_tensor("gvh", [128, 96], F32))
    small = es.enter_context(nc.sbuf_tensor("small", [128, 64], F32))
    ps = [es.enter_context(nc.psum_tensor("ps%d" % i, [128, 512], F32)) for i in range(8)]

    ones_d = matsb[:, 0:128]
    ones_h = matsb[:, 128:256]
    ones_1 = matsb[:, 256:384]
    NU = matsb[:, 384:512]
    negones = matsb[:, 512:640]
    trilmask = matsf[:, 640:768]

    class Arena:
        def __init__(self, t, n):
            self.t, self.n, self.off = t, n, 0

        def reset(self):
            self.off = 0

        def get(self, n):
            o = self.off
            self.off += n
            assert self.off <= self.n, (self.off, self.n)
            return self.t[:, o:o + n]

    A16 = Arena(abf, NBF)
    A32 = Arena(af, NF)

    def DMA(q, out_, in_, reads, writes, key):
        S_.op(q, lambda e: e.dma_start(out=out_, in_=in_), reads, writes, key=key)

    def ACT(out_, in_, func, reads, writes, **kw):
        S_.op("act", lambda e: e.activation(out=out_, in_=in_, func=func, **kw), reads, writes)

    def MM(out_, lhsT, rhs, start, stop, reads, writes, **kw):
        S_.op("pe", lambda e: e.matmul(out_, lhsT=lhsT, rhs=rhs, start=start, stop=stop, **kw), reads, writes)

    def TR(out_, in_, reads, writes):
        S_.op("pe", lambda e: e.transpose(out=out_, in_=in_, identity=ident[:]), list(reads) + ["consts"], writes)

    def TT(eng, out_, in0, in1, op, reads, writes):
        S_.op(eng, lambda e: e.tensor_tensor(out=out_, in0=in0, in1=in1, op=op), reads, writes)

    def STT(eng, out_, in0, scalar, in1, op0, op1, reads, writes):
        S_.op(eng, lambda e: e.scalar_tensor_tensor(out=out_, in0=in0, scalar=scalar, in1=in1, op0=op0, op1=op1), reads, writes)

    def TS(eng, out_, in0, s1, s2, op0, op1, reads, writes):
        if s2 is None:
            S_.op(eng, lambda e: e.tensor_scalar(out=out_, in0=in0, scalar1=s1, scalar2=None, op0=op0), reads, writes)
        else:
            S_.op(eng, lambda e: e.tensor_scalar(out=out_, in0=in0, scalar1=s1, scalar2=s2, op0=op0, op1=op1), reads, writes)

    def CP(eng, out_, in_, reads, writes):
        S_.op(eng, lambda e: e.tensor_copy(out=out_, in_=in_), reads, writes)

    def RCP(out_, in_, reads, writes):
        S_.op("dve", lambda e: e.reciprocal(out=out_, in_=in_), reads, writes)

    DMA("sp", ident[:], c_ident, [], ["consts"], "c0")
    DMA("sp", matsf[:], c_mats, [], ["consts"], "c0")
    DMA("sp", gv[:], g_in, [], ["consts"], "c0")
    CP("dve", matsb[:], matsf[:], ["consts"], ["consts"])
    S_.op("dve", lambda e: e.tensor_scalar(out=gvh[:], in0=gv[:], scalar1=0.5, scalar2=None, op0=ALU.mult), ["consts"], ["consts"])

    def gcol(l, i, c):
        k = (l * 6 + i) * 8 + c
        return gv[:, k:k + 1]

    def gcolh(l, i, c):
        k = (l * 6 + i) * 8 + c
        return gvh[:, k:k + 1]

    def rms_stats(src, nch, ones_t, w, src_res, bufs, tag):
        sq, rt, rstd = bufs
        for c in range(nch):
            sqb = sq[c % 2]
            ACT(sqb[:, 0:w], src(c), AF.Square, [src_res], [("sq", c % 2)])
            MM(ps[6][:, 0:w], ones_t, sqb[:, 0:w], c == 0, c == nch - 1, [("sq", c % 2), "consts"], [("ps", 6)])
        ACT(rt[:, 0:w], ps[6][:, 0:w], AF.Sqrt, [("ps", 6), "consts"], [("rt", tag)], bias=small[:, 0:1], scale=1.0)
        RCP(rstd[:, 0:w], rt[:, 0:w], [("rt", tag)], [("rstd", tag)])
        return rstd

    S_.op("dve", lambda e: e.memset(small[:, 0:1], EPS), [], ["consts"])

    def load_weight(dst, src_ap, nchunk, ncols, name):
        WCH = 512
        for c in range(nchunk):
            for o in range(0, ncols, WCH):
                w_ = min(WCH, ncols - o)
                DMA("pool", dst[:, c * ncols + o:c * ncols + o + w_], src_ap[c * 128:(c + 1) * 128, o:o + w_], [], [("w", name)], ("w", name))

    def ffn_phase(l, j, src, dst, ntok, src_tok=False, dst_tok=False):
        S_.fence()
        A16.reset(); A32.reset()
        Wg = A16.get(KC * FF); Wu = A16.get(KC * FF); Wd = A16.get(FC * D)
        hT = [A16.get(KC * TB) for _ in range(2)]
        aT = A16.get(FC * TB)
        sq = [A16.get(TB) for _ in range(2)]
        sg = [A16.get(TB) for _ in range(2)]
        xT = [A32.get(KC * TB) for _ in range(2)]
        fT = A32.get(KC * TB)
        rt = A32.get(TB); rstd = A32.get(TB); rt2 = A32.get(TB); rstd2 = A32.get(TB)
        xtok = A32.get(NSUB * D) if (src_tok or dst_tok) else None
        wi = l * 2 + j
        load_weight(Wg, wgate[wi], KC, FF, "Wg")
        load_weight(Wu, wup[wi], KC, FF, "Wu")
        load_weight(Wd, wdown[wi], FC, D, "Wd")
        ipre, ipost = (0, 1) if j == 0 else (4, 5)
        nblk = ntok // TB
        misc = [4, 5, 7]
        mi = [0]

        def nextmisc():
            b = misc[mi[0] % 3]
            mi[0] += 1
            return b

        def S1(n):
            sl = n % 2
            xs = xT[sl]
            xres = ("xT", sl)
            if src_tok:
                for s in range(NSUB):
                    DMA("sp", xtok[:, s * D:(s + 1) * D], src[n * TB + s * 128:n * TB + (s + 1) * 128, :], [], ["xtok"], "xtok")
                for c in range(KC):
                    b = nextmisc()
                    for s in range(NSUB):
                        TR(ps[b][:, s * 128:(s + 1) * 128], xtok[:, s * D + c * 128:s * D + (c + 1) * 128], ["xtok"], [("ps", b)])
                    CP("dve", xs[:, c * TB:(c + 1) * TB], ps[b][:, 0:TB], [("ps", b)], [xres])
            else:
                for c in range(KC):
                    DMA("sp", xs[:, c * TB:(c + 1) * TB], src[c * 128:(c + 1) * 128, n * TB:(n + 1) * TB], [], [xres], xres)
            r1 = rms_stats(lambda c: xs[:, c * TB:(c + 1) * TB], KC, ones_d, TB, xres, (sq, rt, rstd), "a")
            for c in range(KC):
                STT("dve", hT[sl][:, c * TB:(c + 1) * TB], xs[:, c * TB:(c + 1) * TB], gcol(l, ipre, c), r1[:, 0:TB],
                    ALU.mult, ALU.mult, [xres, ("rstd", "a"), "consts"], [("hT", sl)])
        def S2(n):
            sl = n % 2
            for f in range(FC):
                bg = f % 2
                bu = 2 + f % 2
                for c in range(KC):
                    MM(ps[bg][:, 0:TB], Wg[:, c * FF + f * 128:c * FF + (f + 1) * 128], hT[sl][:, c * TB:(c + 1) * TB],
                       c == 0, c == KC - 1, [("w", "Wg"), ("hT", sl)], [("ps", bg)])
                for c in range(KC):
                    MM(ps[bu][:, 0:TB], Wu[:, c * FF + f * 128:c * FF + (f + 1) * 128], hT[sl][:, c * TB:(c + 1) * TB],
                       c == 0, c == KC - 1, [("w", "Wu"), ("hT", sl)], [("ps", bu)])
                ACT(sg[f % 2][:, 0:TB], ps[bg][:, 0:TB], AF.Silu, [("ps", bg)], [("sg", f % 2)])
                TT("dve", aT[:, f * TB:(f + 1) * TB], ps[bu][:, 0:TB], sg[f % 2][:, 0:TB], ALU.mult,
                   [("ps", bu), ("sg", f % 2)], [("aT", f)])
        def S3(n):
            for c in range(KC):
                b = nextmisc()
                for f in range(FC):
                    MM(ps[b][:, 0:TB], Wd[:, f * D + c * 128:f * D + (c + 1) * 128], aT[:, f * TB:(f + 1) * TB],
                       f == 0, f == FC - 1, [("w", "Wd"), ("aT", f)], [("ps", b)])
                ACT(fT[:, c * TB:(c + 1) * TB], ps[b][:, 0:TB], AF.Copy, [("ps", b)], ["fT"])
        def S4(n):
            sl = n % 2
            xs = xT[sl]
            xres = ("xT", sl)
            r2 = rms_stats(lambda c: fT[:, c * TB:(c + 1) * TB], KC, ones_d, TB, "fT", (sq, rt2, rstd2), "b")
            for c in range(KC):
                STT("dve", fT[:, c * TB:(c + 1) * TB], fT[:, c * TB:(c + 1) * TB], gcolh(l, ipost, c), r2[:, 0:TB],
                    ALU.mult, ALU.mult, ["fT", ("rstd", "b"), "consts"], ["fT"])
                TT("dve", xs[:, c * TB:(c + 1) * TB], fT[:, c * TB:(c + 1) * TB], xs[:, c * TB:(c + 1) * TB], ALU.add,
                   ["fT", xres], [xres])
            if dst_tok:
                for s in range(NSUB):
                    for hb in range(2):
                        b = nextmisc()
                        for cc in range(4):
                            c = hb * 4 + cc
                            TR(ps[b][:, cc * 128:(cc + 1) * 128], xs[:, c * TB + s * 128:c * TB + (s + 1) * 128], [xres], [("ps", b)])
                        CP("dve", xtok[:, s * D + hb * 512:s * D + (hb + 1) * 512], ps[b][:, 0:512], [("ps", b)], ["xtok"])
                    DMA("sp", dst[n * TB + s * 128:n * TB + (s + 1) * 128, :], xtok[:, s * D:(s + 1) * D], ["xtok"], ["xtok_st"], "xtok_st")
                S_.op("sp", lambda e: None, ["xtok_st"], ["xtok"])
            else:
                for c in range(KC):
                    DMA("sp", dst[c * 128:(c + 1) * 128, n * TB:(n + 1) * TB], xs[:, c * TB:(c + 1) * TB], [xres], [("xst", sl)], ("xst", sl))
                S_.op("sp", lambda e: None, [("xst", sl)], [xres])

        S1(0)
        for n in range(nblk):
            S2(n)
            if n + 1 < nblk:
                S1(n + 1)
            S3(n)
            S4(n)

    def proj_phase():
        S_.fence()
        A16.reset(); A32.reset()
        Win = A16.get(KC * 3072)
        hT = [A16.get(KC * TB) for _ in range(2)]
        sq = [A16.get(TB) for _ in range(2)]
        kst = [A16.get(KC * TB) for _ in range(2)]
        qst = [A16.get(KC * TB) for _ in range(2)]
        vst = [A16.get(NSUB * D) for _ in range(2)]
        xT = [A32.get(KC * TB) for _ in range(2)]
        rt = A32.get(TB); rstd = A32.get(TB)
        load_weight(Win, w_in, KC, 3072, "Win")
        nblk = S_all // TB
        bi = [0]

        def nb():
            b = bi[0] % 6
            bi[0] += 1
            return b

        def PL(n):
            sl = n % 2
            xs = xT[sl]
            xres = ("xT", sl)
            for c in range(KC):
                DMA("sp", xs[:, c * TB:(c + 1) * TB], x1T[c * 128:(c + 1) * 128, n * TB:(n + 1) * TB], [], [xres], xres)
            r1 = rms_stats(lambda c: xs[:, c * TB:(c + 1) * TB], KC, ones_d, TB, xres, (sq, rt, rstd), "a")
            for c in range(KC):
                STT("dve", hT[sl][:, c * TB:(c + 1) * TB], xs[:, c * TB:(c + 1) * TB], gcol(0, 2, c), r1[:, 0:TB],
                    ALU.mult, ALU.mult, [xres, ("rstd", "a"), "consts"], [("hT", sl)])

        def PK(n):
            sl = n % 2
            for kc in range(8):
                col0 = 512 + 128 * kc if kc < 4 else 2048 + 128 * (kc - 4)
                b = nb()
                for c in range(KC):
                    MM(ps[b][:, 0:TB], Win[:, c * 3072 + col0:c * 3072 + col0 + 128], hT[sl][:, c * TB:(c + 1) * TB],
                       c == 0, c == KC - 1, [("w", "Win"), ("hT", sl)], [("ps", b)])
                ACT(kst[sl][:, kc * TB:(kc + 1) * TB], ps[b][:, 0:TB], AF.Copy, [("ps", b)], [("kst", sl)])
            for kc in range(8):
                DMA("sp", kT[kc * 128:(kc + 1) * 128, n * TB:(n + 1) * TB], kst[sl][:, kc * TB:(kc + 1) * TB], [("kst", sl)], [("kst_st", sl)], ("kst_st", sl))
            S_.op("sp", lambda e: None, [("kst_st", sl)], [("kst", sl)])

        def PV_(n):
            sl = n % 2
            for s in range(NSUB):
                for hv in range(2):
                    col0 = 1024 if hv == 0 else 2560
                    b = nb()
                    for c in range(KC):
                        MM(ps[b][:, 0:512], hT[sl][:, c * TB + s * 128:c * TB + (s + 1) * 128], Win[:, c * 3072 + col0:c * 3072 + col0 + 512],
                           c == 0, c == KC - 1, [("w", "Win"), ("hT", sl)], [("ps", b)])
                    CP("dve", vst[sl][:, s * D + hv * 512:s * D + (hv + 1) * 512], ps[b][:, 0:512], [("ps", b)], [("vst", sl)])
            for s in range(NSUB):
                DMA("sp", vS[n * TB + s * 128:n * TB + (s + 1) * 128, :], vst[sl][:, s * D:(s + 1) * D], [("vst", sl)], [("vst_st", sl)], ("vst_st", sl))
            S_.op("sp", lambda e: None, [("vst_st", sl)], [("vst", sl)])
            if n * TB < S_own:
                for kc in range(8):
                    col0 = 128 * kc if kc < 4 else 1536 + 128 * (kc - 4)
                    b = nb()
                    for c in range(KC):
                        MM(ps[b][:, 0:TB], Win[:, c * 3072 + col0:c * 3072 + col0 + 128], hT[sl][:, c * TB:(c + 1) * TB],
                           c == 0, c == KC - 1, [("w", "Win"), ("hT", sl)], [("ps", b)])
                    ACT(qst[sl][:, kc * TB:(kc + 1) * TB], ps[b][:, 0:TB], AF.Copy, [("ps", b)], [("qst", sl)], scale=0.125)
                for kc in range(8):
                    DMA("sp", qT[kc * 128:(kc + 1) * 128, n * TB:(n + 1) * TB], qst[sl][:, kc * TB:(kc + 1) * TB], [("qst", sl)], [("qst_st", sl)], ("qst_st", sl))
                S_.op("sp", lambda e: None, [("qst_st", sl)], [("qst", sl)])

        PL(0)
        for n in range(nblk):
            PK(n)
            if n + 1 < nblk:
                PL(n + 1)
            PV_(n)

    def attn_phase():
        S_.fence()
        A16.reset(); A32.reset()
        kt = [A16.get(S_all) for _ in range(2)]
        vt = [A16.get(NKT * 128) for _ in range(2)]
        qt = [A16.get(S_own) for _ in range(2)]
        mmul = A16.get(4 * 512)
        madd = A16.get(4 * 512)
        Eb = [A16.get(512) for _ in range(4)]
        spb = [A16.get(512) for _ in range(3)]
        wb = [A16.get(512) for _ in range(3)]
        Sacc3 = [A16.get(512) for _ in range(3)]
        osb = [A16.get(512) for _ in range(2)]
        sqh = [A16.get(512) for _ in range(2)]
        ef = [A32.get(512) for _ in range(2)]
        tmpf = [A32.get(512) for _ in range(2)]
        rz = A32.get(512)
        tm = [A32.get(512) for _ in range(2)]
        oa = A32.get(512)
        rt = A32.get(512); rstd = A32.get(512)
        biasT = A32.get(4 * 68)
        biasL = A32.get(4 * 68)
        lamt = A32.get(256)
        neglast = small[:, 1:2]
        neglam = small[:, 2:3]
        gsub = small[:, 3:4]
        DMA("pool", madd, c_maskadd, [], ["aconstb"], "aconstb")
        DMA("pool", mmul, c_maskmul, [], ["aconstb"], "aconstb")
        DMA("sp", biasT, c_bias, [], ["aconst"], "aconst")
        DMA("sp", neglast, c_neglast, [], ["aconst"], "aconst")
        DMA("sp", lamt, lam_in, [], ["aconst"], "aconst")
        DMA("sp", gsub, gsub_in, [], ["aconst"], "aconst")
        TS("dve", biasL, biasT, neglast, None, ALU.add, None, ["aconst"], ["aconst2"])
        p1 = tmpf[0]
        TT("dve", p1[:, 0:64], lamt[:, 0:64], lamt[:, 64:128], ALU.mult, ["aconst"], ["lam_p"])
        S_.op("dve", lambda e: e.reduce_sum(out=small[:, 4:5], in_=p1[:, 0:64], axis=AX.X), ["lam_p"], ["lam_s1"])
        TT("dve", p1[:, 64:128], lamt[:, 128:192], lamt[:, 192:256], ALU.mult, ["aconst"], ["lam_p2"])
        S_.op("dve", lambda e: e.reduce_sum(out=small[:, 5:6], in_=p1[:, 64:128], axis=AX.X), ["lam_p2"], ["lam_s2"])
        ACT(small[:, 6:8], small[:, 4:6], AF.Exp, ["lam_s1", "lam_s2"], ["lam_e"])
        TT("dve", small[:, 8:9], small[:, 7:8], small[:, 6:7], ALU.subtract, ["lam_e"], ["lam_d"])
        TS("dve", neglam, small[:, 8:9], -LAMBDA_INIT0, None, ALU.add, None, ["lam_d"], ["aconst3"])
        TS("dve", small[:, 9:10], gsub, 1.0 - LAMBDA_INIT0, None, ALU.mult, None, ["aconst"], ["aconst4"])
        gsubs = small[:, 9:10]
        for i in range(2):
            S_.op("dve", lambda e, i=i: e.memset(kt[i][64:66, :], 1.0), [], [("kt", i)])

        def tiles(i):
            seq = []
            for ib in range(i, -1, -1):
                for r in range(3, -1, -1):
                    seq.append((ib * 512 + r * 128, 1024 * (i - ib) - 128 * r, "diag" if ib == i else "full", r))
                for r in range(3, -1, -1):
                    seq.append((S_own + ib * 512 + r * 128, 1024 * (i - ib) + 512 - 128 * r, "last" if ib == 0 else "full", r))
            return seq

        cnt = {"sc": 0, "E": 0, "acc": 0, "z": 0, "e": 0, "sp": 0, "w": 0, "tmp": 0, "osb": 0}

        def rot(name, n):
            v = cnt[name] % n
            cnt[name] += 1
            return v

        tm0_all = A32.get(S_own)
        ui = 0
        for h in range(4):
            vsl = h % 2
            for t in range(NKT):
                DMA("sp", vt[vsl][:, t * 128:(t + 1) * 128], vS[t * 128:(t + 1) * 128, h * 128:(h + 1) * 128],
                    [], [("vt", vsl)], ("vt", vsl))
            for m in range(2):
                u = 2 * h + m
                ksl = ui % 2
                ui += 1
                for o_ in range(0, S_all, 2048):
                    DMA("sp", kt[ksl][0:64, o_:min(o_ + 2048, S_all)], kT[u * 64:(u + 1) * 64, o_:min(o_ + 2048, S_all)], [], [("kt", ksl)], ("kt", ksl))
                for o_ in range(0, S_own, 2048):
                    DMA("sp", qt[ksl][0:64, o_:min(o_ + 2048, S_own)], qT[u * 64:(u + 1) * 64, o_:min(o_ + 2048, S_own)], [], [("qt", ksl)], ("qt", ksl))
                for o_ in range(0, S_own, 512):
                    DMA("pool", qt[ksl][64:66, o_:o_ + 512], c_qaug[h][:, o_:o_ + 512], [], [("qt", ksl)], ("qta", ksl))
                for i in range(NO):
                    a = rot("acc", 2)
                    bO, bZ = 4 + a * 2, 5 + a * 2
                    seq = tiles(i)
                    pend = []

                    def pv(last_):
                        eb0, koff0, ti0 = pend.pop(0)
                        MM(ps[bO][:, :], vt[vsl][:, koff0:koff0 + 128], Eb[eb0], ti0 == 0, last_, [("vt", vsl), ("E", eb0)], [("ps", bO)])
                        MM(ps[bZ][:, :], ones_1, Eb[eb0], ti0 == 0, last_, ["consts", ("E", eb0)], [("ps", bZ)])

                    for ti, (koff, d0, kind, r) in enumerate(seq):
                        bs = rot("sc", 3)
                        MM(ps[bs][:, :], kt[ksl][0:66, koff:koff + 128], qt[ksl][0:66, i * 512:(i + 1) * 512], True, True,
                           [("kt", ksl), ("qt", ksl)], [("ps", bs)])
                        eb = rot("E", 4)
                        idx = d0 // 128 + 3
                        bcol = h * 68 + idx
                        if kind == "diag":
                            tb_ = rot("tmp", 2)
                            TT("dve", tmpf[tb_], ps[bs][:, :], madd[:, r * 512:(r + 1) * 512], ALU.add, [("ps", bs), "aconstb"], [("tmpf", tb_)])
                            ACT(Eb[eb], tmpf[tb_], AF.Exp, [("tmpf", tb_), "aconst"], [("E", eb)], bias=biasT[:, bcol:bcol + 1], scale=1.0)
                        elif kind == "last":
                            ACT(Eb[eb], ps[bs][:, :], AF.Exp, [("ps", bs), "aconst2"], [("E", eb)], bias=biasL[:, bcol:bcol + 1], scale=1.0)
                        else:
                            ACT(Eb[eb], ps[bs][:, :], AF.Exp, [("ps", bs), "aconst"], [("E", eb)], bias=biasT[:, bcol:bcol + 1], scale=1.0)
                        pend.append((eb, koff, ti))
                        if len(pend) > 2:
                            pv(False)
                    while len(pend) > 1:
                        pv(False)
                    pv(True)
                    RCP(rz, ps[bZ][:, :], [("ps", bZ)], ["rz"])
                    if m == 0:
                        TT("dve", tm0_all[:, i * 512:(i + 1) * 512], ps[bO][:, :], rz, ALU.mult, [("ps", bO), "rz"], [("tm0", i)])
                    else:
                        TT("dve", tm[0], ps[bO][:, :], rz, ALU.mult, [("ps", bO), "rz"], ["tm1"])
                        STT("dve", oa, tm[0], neglam, tm0_all[:, i * 512:(i + 1) * 512], ALU.mult, ALU.add,
                            ["tm1", ("tm0", i), "aconst3"], ["oa"])
                        ACT(sqh[0], oa, AF.Square, ["oa"], ["sqh"])
                        MM(ps[3][:, :], ones_h, sqh[0], True, True, ["sqh", "consts"], [("ps", 3)])
                        ACT(rt, ps[3][:, :], AF.Sqrt, [("ps", 3)], ["rt_h"], bias=small[:, 0:1], scale=1.0)
                        RCP(rstd, rt, ["rt_h"], ["rstd_h"])
                        ob_ = rot("osb", 2)
                        STT("dve", osb[ob_], oa, gsubs, rstd, ALU.mult, ALU.mult, ["oa", "rstd_h", "aconst4"], [("osb", ob_)])
                        DMA("sp", oT[h * 128:(h + 1) * 128, i * 512:(i + 1) * 512], osb[ob_], [("osb", ob_)], [("osb_st", ob_)], ("osb_st", ob_))
                        S_.op("sp", lambda e: None, [("osb_st", ob_)], [("osb", ob_)])

        Sac = Sacc3
        for h in range(8):
            vsl = h % 2
            ksl = h % 2
            u = 8 + h
            for t in range(NKT):
                DMA("sp", vt[vsl][:, t * 64:(t + 1) * 64], vS[t * 128:(t + 1) * 128, 512 + h * 64:512 + (h + 1) * 64],
                    [], [("vt", vsl)], ("vt", vsl))
            for o_ in range(0, S_all, 2048):
                DMA("sp", kt[ksl][0:64, o_:min(o_ + 2048, S_all)], kT[u * 64:(u + 1) * 64, o_:min(o_ + 2048, S_all)], [], [("kt", ksl)], ("kt", ksl))
            for o_ in range(0, S_own, 2048):
                DMA("sp", qt[ksl][0:64, o_:min(o_ + 2048, S_own)], qT[u * 64:(u + 1) * 64, o_:min(o_ + 2048, S_own)], [], [("qt", ksl)], ("qt", ksl))
            for i in range(NO):
                a = rot("acc", 2)
                bo = 4 + a
                seq = tiles(i)
                jobs = []
                for ti, (koff, d0, kind, r) in enumerate(seq):
                    jobs.append(dict(ti=ti, koff=koff, kind=kind, r=r))

                def stA(j):
                    bs = rot("z", 4)
                    j["bs"] = bs
                    MM(ps[bs][:, :], kt[ksl][0:64, j["koff"]:j["koff"] + 128], qt[ksl][0:64, i * 512:(i + 1) * 512], True, True,
                       [("kt", ksl), ("qt", ksl)], [("ps", bs)])
                    eb = rot("e", 2)
                    sb_ = rot("sp", 3)
                    j["sp"] = sb_
                    if j["kind"] == "last":
                        ACT(ef[eb], ps[bs][:, :], AF.Exp, [("ps", bs), "aconst"], [("ef", eb)], bias=neglast, scale=1.0)
                    else:
                        ACT(ef[eb], ps[bs][:, :], AF.Exp, [("ps", bs)], [("ef", eb)])
                    j["eb"] = eb

                def stA2(j):
                    eb = j["eb"]
                    sb_ = j["sp"]
                    ACT(spb[sb_], ef[eb], AF.Ln, [("ef", eb)], [("spb", sb_)], bias=1.0, scale=1.0)
                    if j["kind"] == "diag":
                        TT("dve", spb[sb_], spb[sb_], mmul[:, j["r"] * 512:(j["r"] + 1) * 512], ALU.mult, [("spb", sb_), "aconstb"], [("spb", sb_)])
                    sa = j["ti"] % 3
                    j["sa"] = sa
                    if j["ti"] == 0:
                        CP("pool", Sac[sa], spb[sb_], [("spb", sb_)], [("Sac", sa)])
                    else:
                        TT("pool", Sac[sa], Sac[(j["ti"] - 1) % 3], spb[sb_], ALU.add, [("Sac", (j["ti"] - 1) % 3), ("spb", sb_)], [("Sac", sa)])

                def stB(j):
                    bs = j["bs"]
                    sb_ = j["sp"]
                    MM(ps[bs][:, :], NU, spb[sb_], False, j["ti"] == 0, ["consts", ("spb", sb_)], [("ps", bs)], skip_group_check=True)
                    if j["ti"] > 0:
                        pa = (j["ti"] - 1) % 3
                        MM(ps[bs][:, :], negones, Sac[pa], False, True, ["consts", ("Sac", pa)], [("ps", bs)], skip_group_check=True)
                    wi_ = rot("w", 3)
                    j["w"] = wi_
                    if j["kind"] == "last":
                        ACT(wb[wi_], ps[bs][:, :], AF.Exp, [("ps", bs), "aconst"], [("wb", wi_)], bias=neglast, scale=1.0)
                    else:
                        ACT(wb[wi_], ps[bs][:, :], AF.Exp, [("ps", bs)], [("wb", wi_)])
                    if j["kind"] == "diag":
                        TT("dve", wb[wi_], wb[wi_], mmul[:, j["r"] * 512:(j["r"] + 1) * 512], ALU.mult, [("wb", wi_), "aconstb"], [("wb", wi_)])

                def stC(j):
                    t = j["koff"] // 128
                    MM(ps[bo][0:64, :], vt[vsl][:, t * 64:(t + 1) * 64], wb[j["w"]], j["ti"] == 0, j["ti"] == len(jobs) - 1,
                       [("vt", vsl), ("wb", j["w"])], [("ps", bo)])

                N = len(jobs)
                for n in range(N + 2):
                    if n < N:
                        stA(jobs[n])
                    if 0 <= n - 1 < N:
                        stB(jobs[n - 1])
                    if n < N:
                        stA2(jobs[n])
                    if 0 <= n - 2 < N:
                        stC(jobs[n - 2])
                    for _f in range(4):
                        MM(ps[7][:, :], ones_1, mmul[:, 0:512], True, True, ["consts", "aconstb"], [("ps", 7)])
                ob_ = rot("osb", 2)
                ACT(osb[ob_][0:64, :], ps[bo][0:64, :], AF.Copy, [("ps", bo)], [("osb", ob_)])
                DMA("sp", oT[512 + h * 64:512 + (h + 1) * 64, i * 512:(i + 1) * 512], osb[ob_][0:64, :], [("osb", ob_)], [("osb_st", ob_)], ("osb_st", ob_))
                S_.op("sp", lambda e: None, [("osb_st", ob_)], [("osb", ob_)])

    def outproj_phase():
        S_.fence()
        A16.reset(); A32.reset()
        Wo = A16.get(KC * D)
        oTt = [A16.get(KC * TB) for _ in range(2)]
        sq = [A16.get(TB) for _ in range(2)]
        xT = [A32.get(KC * TB) for _ in range(2)]
        fT = A32.get(KC * TB)
        rt = A32.get(TB); rstd = A32.get(TB)
        load_weight(Wo, w_out, KC, D, "Wo")
        bi = [0]
        def OL(n):
            sl = n % 2
            for c in range(KC):
                DMA("sp", xT[sl][:, c * TB:(c + 1) * TB], x1T[c * 128:(c + 1) * 128, n * TB:(n + 1) * TB], [], [("xT", sl)], ("xT", sl))
                DMA("sp", oTt[sl][:, c * TB:(c + 1) * TB], oT[c * 128:(c + 1) * 128, n * TB:(n + 1) * TB], [], [("oTt", sl)], ("oTt", sl))

        OL(0)
        for n in range(S_own // TB):
            sl = n % 2
            xs = xT[sl]
            xres = ("xT", sl)
            if n + 1 < S_own // TB:
                OL(n + 1)
            for c2 in range(KC):
                b = bi[0] % 4
                bi[0] += 1
                for c in range(KC):
                    MM(ps[b][:, 0:TB], Wo[:, c * D + c2 * 128:c * D + (c2 + 1) * 128], oTt[sl][:, c * TB:(c + 1) * TB],
                       c == 0, c == KC - 1, [("w", "Wo"), ("oTt", sl)], [("ps", b)])
                ACT(fT[:, c2 * TB:(c2 + 1) * TB], ps[b][:, 0:TB], AF.Copy, [("ps", b)], ["fT"])
            r2 = rms_stats(lambda c: fT[:, c * TB:(c + 1) * TB], KC, ones_d, TB, "fT", (sq, rt, rstd), "b")
            for c in range(KC):
                STT("dve", fT[:, c * TB:(c + 1) * TB], fT[:, c * TB:(c + 1) * TB], gcol(0, 3, c), r2[:, 0:TB],
                    ALU.mult, ALU.mult, ["fT", ("rstd", "b"), "consts"], ["fT"])
                TT("dve", xs[:, c * TB:(c + 1) * TB], fT[:, c * TB:(c + 1) * TB], xs[:, c * TB:(c + 1) * TB], ALU.add,
                   ["fT", xres], [xres])
            for c in range(KC):
                DMA("sp", xa[c * 128:(c + 1) * 128, n * TB:(n + 1) * TB], xs[:, c * TB:(c + 1) * TB], [xres], [("xst", sl)], ("xst", sl))
            S_.op("sp", lambda e: None, [("xst", sl)], [xres])

    def sgu_phase(src, dst):
        S_.fence()
        A16.reset(); A32.reset()
        Wuv = A16.get(KC * 2048)
        Woc = A16.get(KC * D)
        wspb = A16.get(1024)
        hT = [A16.get(KC * TB) for _ in range(2)]
        sq = [A16.get(TB) for _ in range(2)]
        vnb = [A16.get(D) for _ in range(NSUB)]
        gated = A16.get(KC * TB)
        junk = A16.get(D)
        xT = [A32.get(KC * TB) for _ in range(2)]
        fT = A32.get(KC * TB)
        A32b = Arena(abf.bitcast(F32), NBF // 2)
        A32b.off = 20000
        uT = A32b.get(KC * TB)
        vtok = [A32b.get(D) for _ in range(NSUB)]
        rt = A32.get(TB); rstd = A32.get(TB); rt2 = A32.get(TB); rstd2 = A32.get(TB)
        wspf = fT
        tmpg = A32.get(TB)
        load_weight(Wuv, w_uv, KC, 2048, "Wuv")
        load_weight(Woc, w_outc, KC, D, "Woc")
        prm = A32b.get(3 * D + 8 * TB + 8)
        buvr = prm[:, 0:D]; glnr = prm[:, D:2 * D]; blnr = prm[:, 2 * D:3 * D]
        bspt = prm[:, 3 * D:3 * D + 8 * TB]
        buvc = prm[:, 3 * D + 8 * TB:3 * D + 8 * TB + 8]
        DMA("sp", buvr, buv_row, [], ["sconst"], "sconst")
        DMA("sp", glnr, gln_row, [], ["sconst"], "sconst")
        DMA("sp", blnr, bln_row, [], ["sconst"], "sconst")
        DMA("sp", bspt, bsp_in, [], ["sconst"], "sconst")
        DMA("sp", buvc, buv_col, [], ["sconst"], "sconst")
        DMA("sp", wspf[:, 0:1024], wspT_in, [], ["wspf"], "wspf")
        for g in range(8):
            TT("dve", wspb[:, g * 128:(g + 1) * 128], wspf[:, g * 128:(g + 1) * 128], trilmask, ALU.mult, ["wspf", "consts"], ["wspb"])
        S_.op("sp", lambda e: None, ["wspb"], ["fT"])
        sm = A32b.get(16)
        bi = [0]

        def nb():
            b = bi[0] % 6
            bi[0] += 1
            return b

        def GL(n):
            sl = n % 2
            for c in range(KC):
                DMA("sp", xT[sl][:, c * TB:(c + 1) * TB], src[c * 128:(c + 1) * 128, n * TB:(n + 1) * TB], [], [("xT", sl)], ("xT", sl))

        GL(0)
        for n in range(S_own // TB):
            sl = n % 2
            xs = xT[sl]
            xres = ("xT", sl)
            if n + 1 < S_own // TB:
                GL(n + 1)
            r1 = rms_stats(lambda c: xs[:, c * TB:(c + 1) * TB], KC, ones_d, TB, xres, (sq, rt, rstd), "a")
            for c in range(KC):
                STT("dve", hT[sl][:, c * TB:(c + 1) * TB], xs[:, c * TB:(c + 1) * TB], gcol(1, 2, c), r1[:, 0:TB],
                    ALU.mult, ALU.mult, [xres, ("rstd", "a"), "consts"], [("hT", sl)])
            for fc in range(8):
                b = nb()
                for c in range(KC):
                    MM(ps[b][:, 0:TB], Wuv[:, c * 2048 + fc * 128:c * 2048 + (fc + 1) * 128], hT[sl][:, c * TB:(c + 1) * TB],
                       c == 0, c == KC - 1, [("w", "Wuv"), ("hT", sl)], [("ps", b)])
                ACT(uT[:, fc * TB:(fc + 1) * TB], ps[b][:, 0:TB], AF.Gelu_apprx_tanh, [("ps", b), "sconst"], [("uT", fc)],
                    bias=buvc[:, fc:fc + 1], scale=1.0)
            for s in range(NSUB):
                for hv in range(2):
                    b = nb()
                    for c in range(KC):
                        MM(ps[b][:, 0:512], hT[sl][:, c * TB + s * 128:c * TB + (s + 1) * 128],
                           Wuv[:, c * 2048 + 1024 + hv * 512:c * 2048 + 1024 + (hv + 1) * 512],
                           c == 0, c == KC - 1, [("w", "Wuv"), ("hT", sl)], [("ps", b)])
                    TT("dve", vtok[s][:, hv * 512:(hv + 1) * 512], ps[b][:, 0:512], buvr[:, hv * 512:(hv + 1) * 512], ALU.add,
                       [("ps", b), "sconst"], [("vtok", s)])
                ACT(vtok[s], vtok[s], AF.Gelu_apprx_tanh, [("vtok", s)], [("vtok", s)])
                S_.op("dve", lambda e, s=s: e.reduce_sum(out=sm[:, 0:1], in_=vtok[s], axis=AX.X), [("vtok", s)], ["sm0"])
                ACT(junk, vtok[s], AF.Square, [("vtok", s)], ["junk", "sm1"], accum_out=sm[:, 1:2])
                TS("dve", sm[:, 2:3], sm[:, 0:1], 1.0 / 1024, None, ALU.mult, None, ["sm0"], ["sm2"])
                TT("dve", sm[:, 3:4], sm[:, 2:3], sm[:, 2:3], ALU.mult, ["sm2"], ["sm3"])
                STT("dve", sm[:, 4:5], sm[:, 1:2], 1.0 / 1024, sm[:, 3:4], ALU.mult, ALU.subtract, ["sm1", "sm3"], ["sm4"])
                ACT(sm[:, 5:6], sm[:, 4:5], AF.Sqrt, ["sm4"], ["sm5"], bias=small[:, 0:1], scale=1.0)
                RCP(sm[:, 6:7], sm[:, 5:6], ["sm5"], ["sm6"])
                TS("dve", vtok[s], vtok[s], sm[:, 2:3], sm[:, 6:7], ALU.subtract, ALU.mult, [("vtok", s), "sm2", "sm6"], [("vtok", s)])
                TT("dve", vtok[s], vtok[s], glnr, ALU.mult, [("vtok", s), "sconst"], [("vtok", s)])
                TT("dve", vnb[s], vtok[s], blnr, ALU.add, [("vtok", s), "sconst"], [("vnb", s)])
            for g in range(8):
                b = nb()
                for s in range(NSUB):
                    MM(ps[b][:, s * 128:(s + 1) * 128], vnb[s][:, g * 128:(g + 1) * 128], wspb[:, g * 128:(g + 1) * 128], True, True,
                       [("vnb", s), "wspb"], [("ps", b)])
                TT("dve", tmpg, ps[b][:, 0:TB], bspt[:, g * TB:(g + 1) * TB], ALU.add, [("ps", b), "sconst"], ["tmpg"])
                TT("dve", gated[:, g * TB:(g + 1) * TB], tmpg, uT[:, g * TB:(g + 1) * TB], ALU.mult, ["tmpg", ("uT", g)], [("gated", g)])
            for c2 in range(KC):
                b = nb()
                for g in range(8):
                    MM(ps[b][:, 0:TB], Woc[:, g * D + c2 * 128:g * D + (c2 + 1) * 128], gated[:, g * TB:(g + 1) * TB],
                       g == 0, g == 7, [("w", "Woc"), ("gated", g)], [("ps", b)])
                ACT(fT[:, c2 * TB:(c2 + 1) * TB], ps[b][:, 0:TB], AF.Copy, [("ps", b)], ["fT"])
            r2 = rms_stats(lambda c: fT[:, c * TB:(c + 1) * TB], KC, ones_d, TB, "fT", (sq, rt2, rstd2), "b")
            for c in range(KC):
                STT("dve", fT[:, c * TB:(c + 1) * TB], fT[:, c * TB:(c + 1) * TB], gcol(1, 3, c), r2[:, 0:TB],
                    ALU.mult, ALU.mult, ["fT", ("rstd", "b"), "consts"], ["fT"])
                TT("dve", xs[:, c * TB:(c + 1) * TB], fT[:, c * TB:(c + 1) * TB], xs[:, c * TB:(c + 1) * TB], ALU.add,
                   ["fT", xres], [xres])
            for c in range(KC):
                DMA("sp", dst[c * 128:(c + 1) * 128, n * TB:(n + 1) * TB], xs[:, c * TB:(c + 1) * TB], [xres], [("xst", sl)], ("xst", sl))
            S_.op("sp", lambda e: None, [("xst", sl)], [xres])

    if upto >= 1:
        ffn_phase(0, 0, x_in, x1T, S_all, src_tok=True)
    if upto >= 2:
        proj_phase()
    if upto >= 3:
        attn_phase()
    if upto >= 4:
        outproj_phase()
    if upto >= 5:
        ffn_phase(0, 1, xa, xb, S_own)
    if upto >= 6:
        ffn_phase(1, 0, xb, xa, S_own)
    if upto >= 7:
        sgu_phase(xa, xb)
    if upto >= 8:
        ffn_phase(1, 1, xb, out, S_own, dst_tok=True)
    S_.fence()
    S_.emit()
    es.close()
    return nc


def host_consts(S):
    S_own = S // 2
    TB = 256
    p = np.arange(128)
    mats = np.zeros((128, 6, 128), np.float32)
    mats[:, 0] = 1.0 / 1024
    mats[:, 1] = 1.0 / 128
    mats[:, 2] = 1.0
    mats[:, 3] = -(p[:, None] >= p[None, :]).astype(np.float32)
    mats[:, 4] = -1.0
    mats[:, 5] = (p[:, None] <= p[None, :]).astype(np.float32)
    t = np.arange(512)
    maskadd = np.zeros((128, 4, 512), np.float32)
    maskmul = np.zeros((128, 4, 512), np.float32)
    for r in range(4):
        s_ = 128 * r + p
        maskadd[:, r] = np.where(s_[:, None] <= t[None, :], 0.0, NEG)
        maskmul[:, r] = (s_[:, None] < t[None, :]).astype(np.float32)
    qaug = np.zeros((4, 2, S_own), np.float32)
    tt = np.arange(S_own) % 512
    for h in range(4):
        qaug[h, 0] = -SLOPES[h] * (tt // 16 * 16)
        qaug[h, 1] = -SLOPES[h] * (tt % 16)
    bias = np.zeros((128, 4, 68), np.float32)
    for h in range(4):
        for idx in range(68):
            bias[:, h, idx] = SLOPES[h] * (p - 128.0 * (idx - 3))
    return dict(
        c_ident=np.eye(128, dtype=np.float32),
        c_mats=mats.reshape(128, -1),
        c_maskadd=maskadd.reshape(128, -1),
        c_maskmul=maskmul.reshape(128, -1),
        c_qaug=qaug,
        c_bias=bias.reshape(128, -1),
    )


def make_in_maps(S, x, g_norm, w_ffn_gate, w_ffn_up, w_ffn_down, w_in_ab, w_out_ab, lambda_params, g_subln,
                 w_uv, b_uv, g_sgu_ln, b_sgu_ln, w_spatial, b_spatial, w_out_c):
    B = x.shape[0]
    NB = S // 512
    NO = NB // 2
    TB = 256
    f = lambda a: np.ascontiguousarray(np.asarray(a, dtype=np.float32))
    cst = host_consts(S)
    shared = dict(
        g_in=f(np.asarray(g_norm).reshape(2, 6, 8, 128).transpose(3, 0, 1, 2).reshape(128, 96)),
        w_gate=f(np.asarray(w_ffn_gate).reshape(4, D, FF)),
        w_up=f(np.asarray(w_ffn_up).reshape(4, D, FF)),
        w_down=f(np.asarray(w_ffn_down).reshape(4, FF, D)),
        w_in=f(np.asarray(w_in_ab)[0]),
        w_out=f(np.asarray(w_out_ab)[0]),
        lam_in=f(np.broadcast_to(np.asarray(lambda_params)[0].reshape(1, 256), (128, 256))),
        gsub_in=f(np.asarray(g_subln)[0].reshape(128, 1)),
        w_uv=f(np.asarray(w_uv)[0]),
        buv_col=f(np.asarray(b_uv)[0][:1024].reshape(8, 128).T),
        buv_row=f(np.broadcast_to(np.asarray(b_uv)[0][1024:].reshape(1, 1024), (128, 1024))),
        gln_row=f(np.broadcast_to(np.asarray(g_sgu_ln)[0].reshape(1, 1024), (128, 1024))),
        bln_row=f(np.broadcast_to(np.asarray(b_sgu_ln)[0].reshape(1, 1024), (128, 1024))),
        wspT_in=f(np.asarray(w_spatial)[0].transpose(2, 0, 1).reshape(128, 1024)),
        bsp_in=f(np.broadcast_to(np.tile(np.asarray(b_spatial)[0], (1, TB // 128)).reshape(1, 8 * TB), (128, 8 * TB))),
        w_outc=f(np.asarray(w_out_c)[0]),
        **cst,
    )
    x = np.asarray(x, dtype=np.float32)
    in_maps = []
    for b in range(B):
        xb_ = x[b].reshape(NB, 512, D)
        for p in range(2):
            own = [2 * i + p for i in range(NO)]
            if p == 1:
                other = [2 * i for i in range(NO)]
            else:
                other = [NB - 1] + [2 * i - 1 for i in range(1, NO)]
            xp = np.ascontiguousarray(xb_[own + other].reshape(S, D))
            m = dict(shared)
            m["x_in"] = xp
            m["c_neglast"] = np.full((128, 1), 0.0 if p == 1 else NEG, np.float32)
            in_maps.append(m)
    return in_maps


_CACHE = {}


def run(S, inputs, dbg=False, upto=99):
    key = (S, dbg, upto)
    nc = build(S, dbg=dbg, upto=upto)
    in_maps = make_in_maps(S, **inputs)
    res = run_bass_kernel_spmd(nc, in_maps, core_ids=list(range(len(in_maps))))
    return res


def kernel(**inputs):
    x = np.asarray(inputs["x"])
    B, S, _ = x.shape
    res = run(S, inputs)
    NB = S // 512
    NO = NB // 2
    outp = np.zeros((B, S, D), np.float32)
    k = 0
    for b in range(B):
        ob = outp[b].reshape(NB, 512, D)
        for p in range(2):
            r = np.asarray(res.results[k]["out"]).reshape(NO, 512, D)
            k += 1
            for i in range(NO):
                ob[2 * i + p] = r[i]
    return outp
```
